# Optimizing a Trainium2 kernel written in Bass

```python
import math
import jax, jax.numpy as jnp
from jax import lax
import numpy as np

D_MODEL = 1024
BATCH = 8
SEQ = 2048
DEPTH = 1

MEM_LEN = 256
DN_HEADS = 8
DN_DK = 128
DN_DV = 128
DN_CHUNK = 64
CONV_K = 4
SB_HEADS = 8
SB_DH = 128
SB_BLOCK = 128
MEM_HEADS = 4
MEM_DH = 64
N_BRANCH = 3
NORM_EPS = 1e-6

DN_QK = DN_HEADS * DN_DK
DN_VW = DN_HEADS * DN_DV
DN_QKV_W = 2 * DN_QK + DN_VW
SB_W = SB_HEADS * SB_DH
MEM_W = MEM_HEADS * MEM_DH
IN_SIZES = (DN_QKV_W, DN_VW, DN_HEADS, DN_HEADS, 3 * SB_W, SB_W, MEM_W, MEM_W, N_BRANCH * D_MODEL)
IN_WIDTH = sum(IN_SIZES)

kernel_name = "hybrid_deltanet_stickbreak_memory_block"


def rmsnorm(x, g):
    xf = x.astype(jnp.float32)
    y = xf * lax.rsqrt(jnp.mean(xf * xf, axis=-1, keepdims=True) + NORM_EPS)
    return (y * g.astype(jnp.float32)).astype(x.dtype)


def l2norm(x):
    return x * lax.rsqrt(jnp.sum(x * x, axis=-1, keepdims=True) + NORM_EPS)


def to_heads(t, n_heads):
    b, s, _ = t.shape
    return t.reshape(b, s, n_heads, -1).transpose(0, 2, 1, 3)


def merge_heads(t):
    b, h, s, d = t.shape
    return t.transpose(0, 2, 1, 3).reshape(b, s, h * d)


def causal_dwconv(x, w):
    k = w.shape[0]
    t = x.shape[1]
    xp = jnp.pad(x, ((0, 0), (k - 1, 0), (0, 0)))
    return sum(xp[:, j:j + t] * w[j] for j in range(k))


def gated_delta_rule(q, k, v, beta, g):
    b, h, t, dk = q.shape
    dv = v.shape[-1]
    c = DN_CHUNK
    n = t // c
    q = q.reshape(b, h, n, c, dk)
    k = k.reshape(b, h, n, c, dk)
    v = v.reshape(b, h, n, c, dv)
    beta = beta.reshape(b, h, n, c)
    G = jnp.cumsum(g.reshape(b, h, n, c), axis=-1)
    idx = jnp.arange(c)
    incl = idx[:, None] >= idx[None, :]
    strict = idx[:, None] > idx[None, :]
    diff = G[..., :, None] - G[..., None, :]
    gam_incl = jnp.exp(jnp.where(incl, diff, -jnp.inf))
    gam_strict = jnp.where(strict, gam_incl, 0.0)
    kk = jnp.einsum('bhncd,bhnsd->bhncs', k, k)
    m = beta[..., :, None] * kk * gam_strict
    eye = jnp.eye(c, dtype=jnp.float32)
    t_inv = lax.linalg.triangular_solve(eye + m, jnp.broadcast_to(eye, m.shape),
                                        left_side=True, lower=True, unit_diagonal=True)
    u = jnp.einsum('bhncs,bhnsd->bhncd', t_inv, v * beta[..., None])
    w = jnp.einsum('bhncs,bhnsd->bhncd', t_inv, k * (beta * jnp.exp(G))[..., None])
    a_intra = jnp.einsum('bhncd,bhnsd->bhncs', q, k) * gam_incl
    q_dec = q * jnp.exp(G)[..., None]
    last = G[..., -1]
    k_dec = k * jnp.exp(last[..., None] - G)[..., None]

    def step(s, xs):
        q_n, w_n, u_n, k_n, a_n, last_n = xs
        v_new = u_n - jnp.einsum('bhcd,bhde->bhce', w_n, s)
        o = jnp.einsum('bhcd,bhde->bhce', q_n, s) + jnp.einsum('bhcs,bhse->bhce', a_n, v_new)
        s = s * jnp.exp(last_n)[..., None, None] + jnp.einsum('bhcd,bhce->bhde', k_n, v_new)
        return s, o

    xs = (jnp.moveaxis(q_dec, 2, 0), jnp.moveaxis(w, 2, 0), jnp.moveaxis(u, 2, 0),
          jnp.moveaxis(k_dec, 2, 0), jnp.moveaxis(a_intra, 2, 0), jnp.moveaxis(last, 2, 0))
    s0 = jnp.zeros((b, h, dk, dv), jnp.float32)
    _, o = lax.scan(step, s0, xs)
    return jnp.moveaxis(o, 0, 2).reshape(b, h, t, dv)


def stick_breaking_attention(q, k, v):
    _, _, t, d = q.shape
    scale = 1.0 / math.sqrt(d)
    outs = []
    for i in range(t // SB_BLOCK):
        t0 = i * SB_BLOCK
        kl = t0 + SB_BLOCK
        z = jnp.einsum('bhtd,bhsd->bhts', q[:, :, t0:kl], k[:, :, :kl]).astype(jnp.float32) * scale
        t_pos = t0 + jnp.arange(SB_BLOCK)
        s_pos = jnp.arange(kl)
        causal = s_pos[None, :] < t_pos[:, None]
        log_beta = jax.nn.log_sigmoid(z)
        log_fail = jnp.where(causal, jax.nn.log_sigmoid(-z), 0.0)
        surv = lax.cumsum(log_fail, axis=3, reverse=True) - log_fail
        att = jnp.where(causal, jnp.exp(log_beta + surv), 0.0)
        outs.append(jnp.einsum('bhts,bhsd->bhtd', att.astype(v.dtype), v[:, :, :kl]))
    return jnp.concatenate(outs, axis=2)


def memory_attention(q, mk, mv):
    s = jnp.einsum('bhtd,bhmd->bhtm', q, mk).astype(jnp.float32) * (1.0 / math.sqrt(q.shape[-1]))
    p = jax.nn.softmax(s, axis=-1)
    return jnp.einsum('bhtm,bhmd->bhtd', p.astype(mv.dtype), mv)


def hybrid_layer(x, mem, norm_g, mem_norm_g, w_in, conv_w, a_log, dt_bias, dn_norm_g,
                 w_mem_kv, w_br_dn, w_br_sb, w_br_mem, w_out):
    h = rmsnorm(x, norm_g)
    proj = h @ w_in
    splits = [int(s) for s in np.cumsum(IN_SIZES)[:-1]]
    dn_qkv, dn_z, dn_b, dn_a, sb_qkv, sb_z, m_q, m_z, gates = jnp.split(proj, splits, axis=-1)

    dn_qkv = jax.nn.silu(causal_dwconv(dn_qkv, conv_w))
    dq, dk, dv = jnp.split(dn_qkv, [DN_QK, 2 * DN_QK], axis=-1)
    dq = l2norm(to_heads(dq, DN_HEADS).astype(jnp.float32)) * (DN_DK ** -0.5)
    dk = l2norm(to_heads(dk, DN_HEADS).astype(jnp.float32))
    dv = to_heads(dv, DN_HEADS).astype(jnp.float32)
    beta = jax.nn.sigmoid(dn_b.astype(jnp.float32)).transpose(0, 2, 1)
    g = -(jnp.exp(a_log.astype(jnp.float32))
          * jax.nn.softplus(dn_a.astype(jnp.float32) + dt_bias.astype(jnp.float32))).transpose(0, 2, 1)
    o_dn = gated_delta_rule(dq, dk, dv, beta, g)
    o_dn = merge_heads(rmsnorm(o_dn, dn_norm_g)).astype(x.dtype) * jax.nn.silu(dn_z)

    sq, sk, sv = jnp.split(sb_qkv, 3, axis=-1)
    o_sb = stick_breaking_attention(to_heads(sq, SB_HEADS), to_heads(sk, SB_HEADS), to_heads(sv, SB_HEADS))
    o_sb = merge_heads(o_sb) * jax.nn.silu(sb_z)

    mkv = rmsnorm(mem, mem_norm_g) @ w_mem_kv
    mk, mv = jnp.split(mkv, 2, axis=-1)
    o_m = memory_attention(to_heads(m_q, MEM_HEADS), to_heads(mk, MEM_HEADS), to_heads(mv, MEM_HEADS))
    o_m = merge_heads(o_m) * jax.nn.silu(m_z)

    g_dn, g_sb, g_m = jnp.split(jax.nn.sigmoid(gates), N_BRANCH, axis=-1)
    merged = g_dn * (o_dn @ w_br_dn) + g_sb * (o_sb @ w_br_sb) + g_m * (o_m @ w_br_mem)
    return x + merged @ w_out


def setup_inputs(seed: int = 0) -> dict:
    key = jax.random.key(seed)
    ks = jax.random.split(key, 16)
    f = jnp.float32

    def dense(k, shape, fan_in):
        return jax.random.normal(k, shape, f) * (fan_in ** -0.5)

    def gain(k, shape):
        return 1.0 + 0.02 * jax.random.normal(k, shape, f)

    x = jax.random.normal(ks[0], (BATCH, SEQ, D_MODEL), f)
    mem = jax.random.normal(ks[1], (BATCH, MEM_LEN, D_MODEL), f)
    norm_g = gain(ks[2], (DEPTH, D_MODEL))
    mem_norm_g = gain(ks[3], (DEPTH, D_MODEL))
    w_in = dense(ks[4], (DEPTH, D_MODEL, IN_WIDTH), D_MODEL)
    conv_w = dense(ks[5], (DEPTH, CONV_K, DN_QKV_W), CONV_K)
    a_log = jnp.log(jax.random.uniform(ks[6], (DEPTH, DN_HEADS), f, 1.0, 16.0))
    dt = jnp.exp(jax.random.uniform(ks[7], (DEPTH, DN_HEADS), f, math.log(1e-3), math.log(1e-1)))
    dt_bias = dt + jnp.log(-jnp.expm1(-dt))
    dn_norm_g = gain(ks[8], (DEPTH, DN_DV))
    w_mem_kv = dense(ks[9], (DEPTH, D_MODEL, 2 * MEM_W), D_MODEL)
    w_br_dn = dense(ks[10], (DEPTH, DN_VW, D_MODEL), DN_VW)
    w_br_sb = dense(ks[11], (DEPTH, SB_W, D_MODEL), SB_W)
    w_br_mem = dense(ks[12], (DEPTH, MEM_W, D_MODEL), MEM_W)
    w_out = dense(ks[13], (DEPTH, D_MODEL, D_MODEL), D_MODEL)
    final_g = gain(ks[14], (D_MODEL,))
    return {"x": x, "mem": mem, "norm_g": norm_g, "mem_norm_g": mem_norm_g, "w_in": w_in,
            "conv_w": conv_w, "a_log": a_log, "dt_bias": dt_bias, "dn_norm_g": dn_norm_g,
            "w_mem_kv": w_mem_kv, "w_br_dn": w_br_dn, "w_br_sb": w_br_sb, "w_br_mem": w_br_mem,
            "w_out": w_out, "final_g": final_g}


def reference(x, mem, norm_g, mem_norm_g, w_in, conv_w, a_log, dt_bias, dn_norm_g,
              w_mem_kv, w_br_dn, w_br_sb, w_br_mem, w_out, final_g):
    for l in range(DEPTH):
        x = hybrid_layer(x, mem, norm_g[l], mem_norm_g[l], w_in[l], conv_w[l], a_log[l], dt_bias[l],
                         dn_norm_g[l], w_mem_kv[l], w_br_dn[l], w_br_sb[l], w_br_mem[l], w_out[l])
    return rmsnorm(x, final_g)
```

```python
import numpy as np
import concourse.bass as bass
import concourse.mybir as mybir
from concourse.bass_utils import run_bass_kernel_spmd
from contextlib import ExitStack

F32 = mybir.dt.float32
BF16 = mybir.dt.bfloat16
AF = mybir.ActivationFunctionType
ALU = mybir.AluOpType
AX = mybir.AxisListType

ENGS = ('pe', 'act', 'dve', 'pool', 'sp')


class Buf:
    __slots__ = ('name', 'w', 'r', 'const')

    def __init__(self, name, const=False):
        self.name = name
        self.w = None
        self.r = []
        self.const = const


class Prog:
    def __init__(self, nc, es, ndma=48):
        self.nc = nc
        self.streams = {e: [] for e in ENGS}
        self.esem = {e: es.enter_context(nc.semaphore("sem_" + e)) for e in ENGS}
        self.dsem = [es.enter_context(nc.semaphore("dsem%d" % i)) for i in range(ndma)]
        self.dcnt = [0] * ndma
        self.dnext = 0
        self.final = []
        self.nops = 0

    def _deps(self, eng, reads, writes):
        need = {}

        def add(tok):
            if tok is None:
                return
            k, v = tok
            if eng == 'pe' and k == 'pe':
                return
            if need.get(k, -1) < v:
                need[k] = v
        for b in reads:
            add(b.w)
        for b in writes:
            add(b.w)
            for t in b.r:
                add(t)
        return list(need.items())

    def op(self, eng, fn, reads=(), writes=(), sig=True):
        deps = self._deps(eng, reads, writes)
        idx = len(self.streams[eng])
        tok = (eng, idx)
        for b in writes:
            b.w = tok
            b.r = []
        for b in reads:
            if not b.const:
                b.r.append(tok)
        self.streams[eng].append(dict(fn=fn, deps=deps, sig=sig, dma=None))
        self.nops += 1

    def dma(self, eng, out, in_, reads=(), writes=(), final=False, **kw):
        deps = self._deps(eng, reads, writes)
        idx = self.dnext
        self.dnext = (self.dnext + 1) % len(self.dsem)
        key = ('d', idx)
        prev = self.dcnt[idx]
        if prev > 0:
            deps.append((key, prev))
        self.dcnt[idx] += 16
        tok = (key, self.dcnt[idx])
        for b in writes:
            b.w = tok
            b.r = []
        for b in reads:
            if not b.const:
                b.r.append(tok)
        fn = lambda e, out=out, in_=in_, kw=kw: e.dma_start(out=out, in_=in_, **kw)
        self.streams[eng].append(dict(fn=fn, deps=deps, sig=False, dma=idx))
        if final:
            self.final.append(tok)
        self.nops += 1

    def barrier(self):
        toks = [(e, len(self.streams[e]) - 1) for e in ENGS if self._last_sig(e) is not None]
        toks = [(e, self._last_sig(e)) for e, _ in toks]
        toks += [(('d', i), self.dcnt[i]) for i in range(len(self.dsem)) if self.dcnt[i] > 0]
        for e in ENGS:
            self.streams[e].append(dict(fn=None, deps=[t for t in toks if t[0] != e], sig=False, dma=None))

    def _last_sig(self, e):
        st = self.streams[e]
        for i in range(len(st) - 1, -1, -1):
            if st[i]['fn'] is not None and st[i]['dma'] is None and st[i]['sig']:
                return i
        return None

    def emit(self):
        nc = self.nc
        self.streams['sp'].append(dict(fn=None, deps=list(self.final), sig=False, dma=None))
        nxt = {}
        for e in ENGS:
            st = self.streams[e]
            r = [None] * len(st)
            cur = None
            for i in range(len(st) - 1, -1, -1):
                if st[i]['fn'] is not None and st[i]['dma'] is None and st[i]['sig']:
                    cur = i
                r[i] = cur
            nxt[e] = r
        needed = {e: set() for e in ENGS}
        for e in ENGS:
            for ent in self.streams[e]:
                for k, v in ent['deps']:
                    if not isinstance(k, tuple):
                        t = nxt[k][v]
                        assert t is not None, "dependency on an op that is never signalled"
                        needed[k].add(t)
        count = {}
        for e in ENGS:
            for rank, i in enumerate(sorted(needed[e])):
                count[(e, i)] = rank + 1
        self.n_incs = sum(len(v) for v in needed.values())
        with nc.Block() as block:
            def run(name):
                def f(eng):
                    waited = {}
                    for i, ent in enumerate(self.streams[name]):
                        for k, v in ent['deps']:
                            if isinstance(k, tuple):
                                sem, val = self.dsem[k[1]], v
                            else:
                                sem, val = self.esem[k], count[(k, nxt[k][v])]
                            if waited.get(k, 0) < val:
                                waited[k] = val
                                eng.wait_ge(sem, val)
                        if ent['fn'] is None:
                            continue
                        inst = ent['fn'](eng)
                        if ent['dma'] is not None:
                            inst.then_inc(self.dsem[ent['dma']], 16)
                        elif i in needed[name]:
                            inst.then_inc(self.esem[name], 1)
                return f
            block.tensor(run('pe'))
            block.scalar(run('act'))
            block.vector(run('dve'))
            block.gpsimd(run('pool'))
            block.sync(run('sp'))


T = 2048
D = 1024
NKC = 8
NT = 16
NTG = 4
NEGBIG = -30000.0
EPS = 1e-6

BLK_DNQ, BLK_DNK, BLK_DNV, BLK_DNZ = 0, 8, 16, 24
BLK_SBQ, BLK_SBK, BLK_SBV, BLK_SBZ = 32, 40, 48, 56
BLK_MQ, BLK_MZ = 64, 66
BLK_GDN, BLK_GSB, BLK_GM = 68, 76, 84
NBLK = 92


class StopBuild(Exception):
    pass


def build_nc(stage=99, dbg_n=0, sub=0, skip=()):
    nc = bass.Bass("TRN2", target_bir_lowering=False)

    def din(name, shape):
        return nc.dram_tensor(name, list(shape), F32, kind="ExternalInput").ap()

    d_xT = din("xT", [128, NKC * T])
    d_x = din("x", [T, D])
    d_memT = din("memT", [128, NKC * 256])
    d_wmain = din("wmain", [NBLK, 128, 1024])
    d_wba = din("wba", [128, NKC * 16])
    d_wmkv = din("wmkv", [128, NKC * 512])
    d_wbrdn = din("wbrdn", [8, 128, 1024])
    d_wbrsb = din("wbrsb", [8, 128, 1024])
    d_wbrm = din("wbrm", [8, 128, 256])
    d_wout = din("wout", [128, NKC * 1024])
    d_convw = din("convw", [128, 96])
    d_normg = din("normg", [128, 8])
    d_memg = din("memg", [128, 8])
    d_fing = din("fing", [1, D])
    d_dnng = din("dnng", [128, 1])
    d_alog = din("alog", [8, 1])
    d_dtb = din("dtb", [8, 1])
    d_ident = din("ident", [128, 128])
    d_maskb = din("maskb", [128, 128])
    d_offd = din("offd", [128, 128])
    d_causal = din("causal", [128, 128])
    d_esel = din("esel", [8, 8 * 128])
    d_rmask = din("rmask", [8, 512])
    d_out = nc.dram_tensor("out", [T, D], F32, kind="ExternalOutput").ap()
    d_dbg = None
    if dbg_n:
        d_dbg = nc.dram_tensor("dbg", [128, dbg_n], F32, kind="ExternalOutput").ap()

    with ExitStack() as es:
        P = Prog(nc, es, ndma=48)

        def sb(name, shape, dt):
            return es.enter_context(nc.sbuf_tensor("s_" + name, list(shape), dt))

        def mm(out, lhsT, rhs, start, stop, reads, writes, sig=None):
            if sig is None:
                sig = stop
            P.op('pe', lambda e: e.matmul(out, lhsT, rhs, start=start, stop=stop), reads, writes, sig=sig)

        def tr(out, in_, ident, reads, writes, sig=True):
            P.op('pe', lambda e: e.transpose(out, in_, ident), reads, writes, sig=sig)

        def act(out, in_, func, reads, writes, scale=None, bias=None, accum=None):
            kw = {}
            if scale is not None:
                kw['scale'] = scale
            if bias is not None:
                kw['bias'] = bias
            if accum is not None:
                kw['accum_out'] = accum
            P.op('act', lambda e: e.activation(out, in_, func, **kw), reads, writes)

        def tt(eng, out, a, b, op, reads, writes):
            P.op(eng, lambda e: e.tensor_tensor(out, a, b, op), reads, writes)

        def ts(eng, out, a, s1, s2, op0, op1, reads, writes):
            if op1 is None:
                P.op(eng, lambda e: e.tensor_scalar(out, a, s1, None, op0), reads, writes)
            else:
                P.op(eng, lambda e: e.tensor_scalar(out, a, s1, s2, op0, op1), reads, writes)

        def stt(out, a, scalar, b, op0, op1, reads, writes):
            P.op('dve', lambda e: e.scalar_tensor_tensor(out, a, scalar, b, op0, op1), reads, writes)

        def cp(eng, out, in_, reads, writes):
            P.op(eng, lambda e: e.tensor_copy(out, in_), reads, writes)

        def memset(eng, ap, val, writes):
            P.op(eng, lambda e: e.memset(ap, val), (), writes)

        def barrier():
            P.barrier()

        ps = [es.enter_context(nc.psum_tensor("ps%d" % i, [128, 512], F32)) for i in range(8)]
        psB = [Buf("ps%d" % i) for i in range(8)]

        hT = sb("hT", [128, NKC, T], BF16)
        BhT = Buf("hT")
        d_oTdn = nc.dram_tensor("oTdn_scr", [8, 128, T], BF16).ap()
        d_oTsb = nc.dram_tensor("oTsb_scr", [8, 128, T], BF16).ap()
        oT_m = sb("oT_m", [128, 2, T], BF16)
        BoT_dn = [Buf("oTdn%d" % h) for h in range(8)]
        BoT_sb = [Buf("oTsb%d" % h) for h in range(8)]
        BoT_m = Buf("oTm")
        identf = sb("identf", [128, 128], F32)
        identb = sb("identb", [128, 128], BF16)
        onesf = sb("onesf", [128, 128], F32)
        onesb = sb("onesb", [128, 128], BF16)
        eselb = sb("eselb", [8, 8 * 128], BF16)
        maskbf = sb("maskbf", [128, 128], F32)
        maskbb = sb("maskbb", [128, 128], BF16)
        offdf = sb("offdf", [128, 128], F32)
        offdb = sb("offdb", [128, 128], BF16)
        causf = sb("causf", [128, 128], F32)
        causb = sb("causb", [128, 128], BF16)
        esel = sb("esel", [8, 8 * 128], F32)
        rmask = sb("rmask", [8, 512], F32)
        convw = sb("convw", [128, 96], F32)
        normg = sb("normg", [128, 8], F32)
        memg = sb("memg", [128, 8], F32)
        fing = sb("fing", [128, D], F32)
        dnng = sb("dnng", [128, 1], F32)
        alog = sb("alog", [8, 1], F32)
        dtb = sb("dtb", [8, 1], F32)
        NWB = 11
        wbf = [sb("wbf%d" % i, [128, NKC, 128], BF16) for i in range(NWB)]
        Bwbf = [Buf("wbf%d" % i) for i in range(NWB)]
        wctr = [0, 0]
        AR_BYTES = 122 * 1024
        arena = sb("arena", [128, AR_BYTES // 4], F32)

        def aview(off, shape, dt):
            n = 1
            for s_ in shape[1:]:
                n *= s_
            esz = 4 if dt == F32 else 2
            assert off % 4 == 0 and (n * esz) % 4 == 0
            assert off + n * esz <= AR_BYTES, (off, shape)
            v = arena[:, off // 4: off // 4 + (n * esz) // 4]
            if dt != F32:
                v = v.bitcast(dt)
            if len(shape) == 3:
                v = v.rearrange("p (a b) -> p a b", a=shape[1])
            return v[0:shape[0]] if shape[0] != 128 else v

        dbg_items = []
        dbg_off = [0]
        dbgst = sb("dbgst", [128, 512], F32) if dbg_n else None
        Bdbg = Buf("dbgst")

        def dump(name, ap, n, reads):
            rows = ap.shape[0]
            for c0 in range(0, n, 512):
                cn = min(512, n - c0)
                cp('dve', dbgst[0:rows, 0:cn], ap[:, c0:c0 + cn], list(reads), [Bdbg])
                P.dma('sp', d_dbg[0:rows, dbg_off[0] + c0:dbg_off[0] + c0 + cn], dbgst[0:rows, 0:cn], reads=[Bdbg], final=True)
            dbg_items.append((name, dbg_off[0], rows, n))
            dbg_off[0] += n
            assert dbg_off[0] <= dbg_n

        def load_w(src):
            bi = wctr[1] % NWB
            wctr[1] += 1
            P.dma('pool', wbf[bi][:].rearrange("p a b -> p (a b)"), src, writes=[Bwbf[bi]])
            return wbf[bi], Bwbf[bi]

        def proj_fm(w, Bw, tg, pst, Bps, ncols=512, t0=None):
            if t0 is None:
                t0 = tg * 512
            for kc in range(NKC):
                mm(pst[:, 0:ncols], w[:, kc, :], hT[:, kc, t0:t0 + ncols], kc == 0, kc == NKC - 1,
                   [Bw, BhT], [Bps])

        silu_tmp = [sb("silutmp%d" % i, [128, 512], F32) for i in range(2)]
        Bsilu_tmp = [Buf("silutmp%d" % i) for i in range(2)]
        sctr = [0]

        def silu2(out, y, ncols, reads, writes):
            i = sctr[0] % 2
            sctr[0] += 1
            act(silu_tmp[i][:, 0:ncols], y, AF.Tanh, reads, [Bsilu_tmp[i]], scale=0.5)
            stt(out, silu_tmp[i][:, 0:ncols], 1.0, y, ALU.add, ALU.mult, list(reads) + [Bsilu_tmp[i]], writes)

        CB = {}

        def cb(name):
            if name not in CB:
                CB[name] = Buf(name, const=True)
            return CB[name]

        for (nm, t_, d_) in [("identf", identf, d_ident), ("maskbf", maskbf, d_maskb), ("offdf", offdf, d_offd),
                             ("causf", causf, d_causal), ("esel", esel, d_esel), ("rmask", rmask, d_rmask),
                             ("convw", convw, d_convw), ("normg", normg, d_normg), ("memg", memg, d_memg),
                             ("dnng", dnng, d_dnng), ("alog", alog, d_alog), ("dtb", dtb, d_dtb)]:
            P.dma('sp', t_[:], d_, writes=[cb(nm)])
        P.dma('sp', fing[:], d_fing[0:1, :].partition_broadcast(128), writes=[cb("fing")])
        cp('pool', identb[:], identf[:], [cb("identf")], [cb("identb")])
        cp('pool', maskbb[:], maskbf[:], [cb("maskbf")], [cb("maskbb")])
        cp('pool', offdb[:], offdf[:], [cb("offdf")], [cb("offdb")])
        cp('pool', causb[:], causf[:], [cb("causf")], [cb("causb")])
        memset('pool', onesf[:], 1.0, [cb("onesf")])
        memset('pool', onesb[:], 1.0, [cb("onesb")])
        cp('pool', eselb[:], esel[:], [cb("esel")], [cb("eselb")])

        try:

            if stage == 0:
                dump('identb', identb[:], 128, [cb('identb')])
                dump('fing', fing[:], 1024, [cb('fing')])
            if stage >= 1:
                xc = [aview(0, [128, T], F32), aview(8192, [128, T], F32)]
                sq = [aview(16384, [128, T], BF16), aview(24576, [128, T], BF16)]
                rstd_bc = aview(32768, [128, T], F32)
                Bxc = [Buf("xc0"), Buf("xc1")]
                Bsq = [Buf("sq0"), Buf("sq1")]
                Brstd = Buf("rstd_bc")
                for kc in range(NKC):
                    i = kc % 2
                    P.dma('sp', xc[i][:], d_xT[:, kc * T:(kc + 1) * T], writes=[Bxc[i]])
                    act(sq[i][:], xc[i][:], AF.Square, [Bxc[i]], [Bsq[i]])
                    for tg in range(NTG):
                        mm(ps[tg][:, 0:512], onesb[:], sq[i][:, tg * 512:(tg + 1) * 512], kc == 0, kc == NKC - 1,
                           [cb("onesb"), Bsq[i]], [psB[tg]], sig=True)
                for tg in range(NTG):
                    act(rstd_bc[:, tg * 512:(tg + 1) * 512], ps[tg][:, 0:512], AF.Ln, [psB[tg]], [Brstd],
                        scale=1.0 / D, bias=EPS)
                act(rstd_bc[:], rstd_bc[:], AF.Exp, [Brstd], [Brstd], scale=-0.5)
                for kc in range(NKC):
                    i = kc % 2
                    P.dma('sp', xc[i][:], d_xT[:, kc * T:(kc + 1) * T], writes=[Bxc[i]])
                    stt(hT[:, kc, :], xc[i][:], normg[:, kc:kc + 1], rstd_bc[:], ALU.mult, ALU.mult,
                        [Bxc[i], cb("normg"), Brstd], [BhT])
                if stage == 1:
                    dump("hT0", hT[:, 0, :], T, [BhT])
                    dump("hT7", hT[:, 7, :], T, [BhT])
            barrier()

            if stage >= 2 and 2 not in skip:
                mqT = [aview(4096 * h_, [64, T], BF16) for h_ in range(4)]
                szm = [aview(16384, [128, T], BF16), aview(20480, [128, T], BF16)]
                pexp = [aview(24576, [128, 4, 256], BF16), aview(26624, [128, 4, 256], BF16)]
                pT = [aview(28672, [128, 8, 128], BF16), aview(30720, [128, 8, 128], BF16)]
                om = [aview(32768, [128, 256], BF16), aview(33280, [128, 256], BF16)]
                mstat = [aview(33792, [128, 16], F32), aview(33856, [128, 16], F32)]
                memTc = aview(40960, [128, NKC * 256], F32)
                msq = [aview(49152, [128, 256], BF16), aview(50176, [128, 256], BF16)]
                mrstd = aview(51200, [128, 256], F32)
                memnT = aview(52224, [128, NKC, 256], BF16)
                wmkvf = aview(56320, [128, NKC * 512], F32)
                wmkvb = aview(72704, [128, NKC, 512], BF16)
                mkT = [aview(80896 + 512 * h_, [64, 256], BF16) for h_ in range(4)]
                mv = [aview(82944, [128, 256], BF16), aview(83456, [128, 256], BF16)]
                BmqT = [Buf("mqT%d" % h_) for h_ in range(4)]
                Bszm = [Buf("szm0"), Buf("szm1")]
                Bpexp = [Buf("pexp0"), Buf("pexp1")]
                BpT = [Buf("pT0"), Buf("pT1")]
                Bom = [Buf("om0"), Buf("om1")]
                Bmstat = [Buf("mstat0"), Buf("mstat1")]
                BmemTc, Bmrstd, BmemnT, Bwmkvf, Bwmkvb = Buf("memTc"), Buf("mrstd"), Buf("memnT"), Buf("wmkvf"), Buf("wmkvb")
                Bmsq = [Buf("msq0"), Buf("msq1")]
                BmkT = [Buf("mkT%d" % h_) for h_ in range(4)]
                Bmv = [Buf("mv0"), Buf("mv1")]

                P.dma('sp', memTc[:], d_memT, writes=[BmemTc])
                P.dma('sp', wmkvf[:], d_wmkv, writes=[Bwmkvf])
                for kc in range(NKC):
                    i = kc % 2
                    act(msq[i][:], memTc[:, kc * 256:(kc + 1) * 256], AF.Square, [BmemTc], [Bmsq[i]])
                    mm(ps[0][:, 0:256], onesb[:], msq[i][:], kc == 0, kc == NKC - 1, [cb("onesb"), Bmsq[i]], [psB[0]], sig=True)
                act(mrstd[:], ps[0][:, 0:256], AF.Ln, [psB[0]], [Bmrstd], scale=1.0 / D, bias=EPS)
                act(mrstd[:], mrstd[:], AF.Exp, [Bmrstd], [Bmrstd], scale=-0.5)
                for kc in range(NKC):
                    stt(memnT[:, kc, :], memTc[:, kc * 256:(kc + 1) * 256], memg[:, kc:kc + 1], mrstd[:],
                        ALU.mult, ALU.mult, [BmemTc, cb("memg"), Bmrstd], [BmemnT])
                act(wmkvb[:].rearrange("p a b -> p (a b)"), wmkvf[:], AF.Copy, [Bwmkvf], [Bwmkvb])
                if sub == 1:
                    dump('memnT', memnT[:, 0, :], 256, [BmemnT])
                    raise StopBuild()
                for h_ in range(4):
                    for kc in range(NKC):
                        mm(ps[1][0:64, 0:256], wmkvb[:, kc, h_ * 64:(h_ + 1) * 64], memnT[:, kc, :], kc == 0, kc == NKC - 1,
                           [Bwmkvb, BmemnT], [psB[1]])
                    cp('dve', mkT[h_][:], ps[1][0:64, 0:256], [psB[1]], [BmkT[h_]])
                for mb in range(2):
                    for kc in range(NKC):
                        mm(ps[2][:, 0:256], memnT[:, kc, mb * 128:(mb + 1) * 128], wmkvb[:, kc, 256:512], kc == 0, kc == NKC - 1,
                           [Bwmkvb, BmemnT], [psB[2]])
                    cp('dve', mv[mb][:], ps[2][:, 0:256], [psB[2]], [Bmv[mb]])
                if sub == 2:
                    dump('mkT0', mkT[3][:], 256, [BmkT[3]])
                    dump('mv1', mv[1][:], 256, [Bmv[1]])
                    raise StopBuild()
                pctr = 0
                for hp in range(2):
                    w, Bw = load_w(d_wmain[BLK_MQ + hp])
                    for hh in range(2):
                        h_ = hp * 2 + hh
                        for tg in range(NTG):
                            pi = pctr % 2
                            pctr += 1
                            for kc in range(NKC):
                                mm(ps[pi][0:64, 0:512], w[:, kc, hh * 64:(hh + 1) * 64], hT[:, kc, tg * 512:(tg + 1) * 512],
                                   kc == 0, kc == NKC - 1, [Bw, BhT], [psB[pi]])
                            act(mqT[h_][:, tg * 512:(tg + 1) * 512], ps[pi][0:64, 0:512], AF.Copy, [psB[pi]], [BmqT[h_]])
                    w, Bw = load_w(d_wmain[BLK_MZ + hp])
                    for tg in range(NTG):
                        pi = pctr % 2
                        pctr += 1
                        proj_fm(w, Bw, tg, ps[pi], psB[pi])
                        silu2(szm[hp][:, tg * 512:(tg + 1) * 512], ps[pi][:, 0:512], 512, [psB[pi]], [Bszm[hp]])
                if sub == 3:
                    dump('mqT1', mqT[3][:], 2048, [BmqT[3]])
                    dump('szm0', szm[0][:], 2048, [Bszm[0]])
                    raise StopBuild()
                def memtile(j):
                    b = j % 2
                    tsl = slice(j * 128, (j + 1) * 128)
                    pS = [ps[2 + 2 * b], ps[3 + 2 * b]]
                    BpS = [psB[2 + 2 * b], psB[3 + 2 * b]]
                    for h in range(4):
                        mm(pS[h // 2][:, (h % 2) * 256:(h % 2 + 1) * 256], mqT[h][:, tsl], mkT[h][:, :],
                           True, True, [BmqT[h], BmkT[h]], [BpS[h // 2]])
                    yield
                    mx, nmx, ssum, rs = (mstat[b][:, 0:4], mstat[b][:, 4:8], mstat[b][:, 8:12], mstat[b][:, 12:16])
                    if sub == 41:
                        dump('pS0', pS[0][:, 0:512], 512, [BpS[0]])
                        dump('pS1', pS[1][:, 0:512], 512, [BpS[1]])
                        raise StopBuild()
                    for q in range(2):
                        P.op('dve', (lambda o_, i_: (lambda e: e.tensor_reduce(o_, i_, AX.X, ALU.max)))(
                            mx[:, 2 * q:2 * q + 2], pS[q][:, 0:512].rearrange("p (a b) -> p a b", a=2)),
                            [BpS[q]], [Bmstat[b]])
                    ts('dve', nmx, mx, -0.125, None, ALU.mult, None, [Bmstat[b]], [Bmstat[b]])
                    if sub == 42:
                        dump('mstat', mstat[b][:], 16, [Bmstat[b]])
                        raise StopBuild()
                    yield
                    for h in range(4):
                        act(pexp[b][:, h, :], pS[h // 2][:, (h % 2) * 256:(h % 2 + 1) * 256], AF.Exp,
                            [BpS[h // 2], Bmstat[b]], [Bpexp[b], Bmstat[b]], scale=0.125, bias=nmx[:, h:h + 1],
                            accum=ssum[:, h:h + 1])
                    P.op('dve', (lambda o_, i_: (lambda e: e.reciprocal(o_, i_)))(rs, ssum), [Bmstat[b]], [Bmstat[b]])
                    if sub == 4:
                        dump('mstat', mstat[b][:], 16, [Bmstat[b]])
                        dump('pexp', pexp[b][:].rearrange('p a b -> p (a b)'), 1024, [Bpexp[b]])
                        raise StopBuild()
                    yield
                    pTp = ps[6 + b][:].bitcast(BF16)
                    for h in range(4):
                        for mb in range(2):
                            c0 = (h * 2 + mb) * 128
                            tr(pTp[:, c0:c0 + 128], pexp[b][:, h, mb * 128:(mb + 1) * 128], identb[:],
                               [Bpexp[b], cb("identb")], [psB[6 + b]], sig=(h == 3 and mb == 1))
                    act(pT[b][:].rearrange("p a b -> p (a b)"), pTp[:, 0:1024], AF.Copy, [psB[6 + b]], [BpT[b]])
                    yield
                    pO = ps[b]
                    for h in range(4):
                        for mb in range(2):
                            mm(pO[:, h * 64:(h + 1) * 64], pT[b][:, h * 2 + mb, :], mv[mb][:, h * 64:(h + 1) * 64],
                               mb == 0, mb == 1, [BpT[b], Bmv[mb]], [psB[b]], sig=(h == 3 and mb == 1))
                    tt('dve', om[b][:].rearrange("p (a b) -> p a b", a=4), pO[:, 0:256].rearrange("p (a b) -> p a b", a=4),
                       rs.unsqueeze(2).to_broadcast([128, 4, 64]), ALU.mult, [psB[b], Bmstat[b]], [Bom[b]])
                    if sub == 5:
                        dump('om', om[b][:], 256, [Bom[b]])
                        raise StopBuild()
                    yield
                    pT2 = ps[6 + b][:].bitcast(BF16)
                    for hp in range(2):
                        tr(pT2[:, hp * 128:(hp + 1) * 128], om[b][:, hp * 128:(hp + 1) * 128], identb[:],
                           [Bom[b], cb("identb")], [psB[6 + b]], sig=(hp == 1))
                    for hp in range(2):
                        stt(oT_m[:, hp, tsl], pT2[:, hp * 128:(hp + 1) * 128], 0.5, szm[hp][:, tsl], ALU.mult, ALU.mult,
                            [psB[6 + b], Bszm[hp]], [BoT_m])
                    yield

                for j0 in range(0, NT if sub == 0 else 2, 2):
                    gl = [memtile(j0), memtile(j0 + 1)]
                    while gl:
                        for g_ in list(gl):
                            try:
                                next(g_)
                            except StopIteration:
                                gl.remove(g_)
                if stage == 2:
                    dump("oTm0", oT_m[:, 0, :], T, [BoT_m])
                    dump("oTm1", oT_m[:, 1, :], T, [BoT_m])
                barrier()

            if stage >= 3 and 3 not in skip:
                NRS = 3
                qT = [aview(0, [128, T], BF16), aview(4096, [128, T], BF16)]
                kT = [aview(8192, [128, T], BF16), aview(12288, [128, T], BF16)]
                vTM = [aview(16384, [128, NT, 128], BF16), aview(20480, [128, NT, 128], BF16)]
                sz2 = [aview(24576, [128, T], BF16), aview(28672, [128, T], BF16)]
                oacc = [aview(32768, [128, T], BF16), aview(36864, [128, T], BF16)]
                RB = 40960
                RSZ = 8208 + 8208 + 4096 + 4096 + 16 + 512
                spb = [aview(RB + s_ * RSZ, [128, T + 4], F32) for s_ in range(NRS)]
                Cpad = [aview(RB + s_ * RSZ + 8208, [128, T + 4], F32) for s_ in range(NRS)]
                att = [aview(RB + s_ * RSZ + 16416, [128, T], BF16) for s_ in range(NRS)]
                attT = [aview(RB + s_ * RSZ + 20512, [128, NT, 128], BF16) for s_ in range(NRS)]
                sbst = [aview(RB + s_ * RSZ + 24608, [128, 4], F32) for s_ in range(NRS)]
                junk = [aview(RB + s_ * RSZ + 24624, [128, 128], F32) for s_ in range(NRS)]
                assert RB + NRS * RSZ <= AR_BYTES
                BqT = [Buf("qT0"), Buf("qT1")]
                BkT = [Buf("kT0"), Buf("kT1")]
                BvTM = [Buf("vTM0"), Buf("vTM1")]
                Bsz2 = [Buf("sz20"), Buf("sz21")]
                Boacc = [Buf("oacc0"), Buf("oacc1")]
                Bspb = [Buf("spb%d" % s_) for s_ in range(NRS)]
                BCpad = [Buf("Cpad%d" % s_) for s_ in range(NRS)]
                Batt = [Buf("att%d" % s_) for s_ in range(NRS)]
                BattT = [Buf("attT%d" % s_) for s_ in range(NRS)]
                Bsbst = [Buf("sbst%d" % s_) for s_ in range(NRS)]
                Bjunk = [Buf("junk%d" % s_) for s_ in range(NRS)]
                for s_ in range(NRS):
                    memset('pool', Cpad[s_][:, 0:1], 0.0, [BCpad[s_]])
                    memset('pool', spb[s_][:, 0:1], 0.0, [Bspb[s_]])
                sbc = {'p': 0, 't': 0}

                def sb_proj(h):
                    hb = h % 2
                    wq, Bwq = load_w(d_wmain[BLK_SBQ + h])
                    for tg in range(NTG):
                        pi = sbc['p'] % 2
                        sbc['p'] += 1
                        proj_fm(wq, Bwq, tg, ps[pi], psB[pi])
                        act(qT[hb][:, tg * 512:(tg + 1) * 512], ps[pi][:, 0:512], AF.Copy, [psB[pi]], [BqT[hb]],
                            scale=float(128 ** -0.5))
                        yield
                    wk, Bwk = load_w(d_wmain[BLK_SBK + h])
                    for tg in range(NTG):
                        pi = sbc['p'] % 2
                        sbc['p'] += 1
                        proj_fm(wk, Bwk, tg, ps[pi], psB[pi])
                        cp('dve', kT[hb][:, tg * 512:(tg + 1) * 512], ps[pi][:, 0:512], [psB[pi]], [BkT[hb]])
                        yield
                    wz, Bwz = load_w(d_wmain[BLK_SBZ + h])
                    for tg in range(NTG):
                        pi = sbc['p'] % 2
                        sbc['p'] += 1
                        proj_fm(wz, Bwz, tg, ps[pi], psB[pi])
                        silu2(sz2[hb][:, tg * 512:(tg + 1) * 512], ps[pi][:, 0:512], 512, [psB[pi]], [Bsz2[hb]])
                        yield
                    wv, Bwv = load_w(d_wmain[BLK_SBV + h])
                    for j4 in range(4):
                        pi = sbc['p'] % 2
                        sbc['p'] += 1
                        for jj in range(4):
                            j = j4 * 4 + jj
                            for kc in range(NKC):
                                mm(ps[pi][:, jj * 128:(jj + 1) * 128], hT[:, kc, j * 128:(j + 1) * 128], wv[:, kc, :],
                                   kc == 0, kc == NKC - 1, [BhT, Bwv], [psB[pi]], sig=(kc == NKC - 1 and jj == 3))
                        cp('dve', vTM[hb][:, j4 * 4:(j4 + 1) * 4, :].rearrange("p a b -> p (a b)"), ps[pi][:, 0:512],
                           [psB[pi]], [BvTM[hb]])
                        yield

                def sb_row(h, i, rb):
                    hb = h % 2
                    nk = i + 1
                    L = nk * 128
                    qsl = slice(i * 128, (i + 1) * 128)
                    zb = 2 + rb
                    for c in range((L + 511) // 512):
                        cols = min(512, L - c * 512)
                        c0 = c * 512
                        mm(ps[zb][:, 0:cols], qT[hb][:, qsl], kT[hb][:, c0:c0 + cols], True, True,
                           [BqT[hb], BkT[hb]], [psB[zb]])
                        yield
                        act(spb[rb][:, 1 + c0:1 + c0 + cols], ps[zb][:, 0:cols], AF.Exp, [psB[zb]], [Bspb[rb]], scale=-1.0)
                        act(spb[rb][:, 1 + c0:1 + c0 + cols], spb[rb][:, 1 + c0:1 + c0 + cols], AF.Ln, [Bspb[rb]], [Bspb[rb]], bias=1.0)
                        yield
                        P.op('dve', (lambda o_, d0, d1, ini: (lambda e: e.tensor_tensor_scan(o_, d0, d1, ini, ALU.add, ALU.add)))(
                            Cpad[rb][:, 1 + c0:1 + c0 + cols], spb[rb][:, c0:c0 + cols], ps[zb][:, 0:cols],
                            Cpad[rb][:, c0:c0 + 1]), [Bspb[rb], psB[zb], BCpad[rb]], [BCpad[rb]])
                        yield
                    ctot, nct = sbst[rb][:, 0:1], sbst[rb][:, 1:2]
                    tt('dve', junk[rb][:], Cpad[rb][:, i * 128:i * 128 + 128], spb[rb][:, i * 128:i * 128 + 128], ALU.add,
                       [BCpad[rb], Bspb[rb]], [Bjunk[rb]])
                    P.op('dve', (lambda o_, a_, b_, acc: (lambda e: e.scalar_tensor_tensor(
                        o_, a_, 1.0, b_, ALU.mult, ALU.mult, accum_out=acc)))(
                        junk[rb][:], junk[rb][:], identf[:], ctot),
                        [Bjunk[rb], cb("identf")], [Bjunk[rb], Bsbst[rb]])
                    ts('dve', nct, ctot, -1.0, None, ALU.mult, None, [Bsbst[rb]], [Bsbst[rb]])
                    ts('dve', Cpad[rb][:, 1 + i * 128:1 + L], Cpad[rb][:, 1 + i * 128:1 + L], ctot, None, ALU.min, None,
                       [BCpad[rb], Bsbst[rb]], [BCpad[rb]])
                    yield
                    act(att[rb][:, 0:L], Cpad[rb][:, 1:1 + L], AF.Exp, [BCpad[rb], Bsbst[rb]], [Batt[rb]], bias=nct)
                    yield
                    tt('pool', att[rb][:, i * 128:L], att[rb][:, i * 128:L], causb[:], ALU.mult,
                       [Batt[rb], cb("causb")], [Batt[rb]])
                    yield
                    for g8 in range((nk + 7) // 8):
                        tb = 5 + (sbc['t'] % 2)
                        sbc['t'] += 1
                        n8 = min(8, nk - g8 * 8)
                        ptb = ps[tb][:].bitcast(BF16)
                        for m8 in range(n8):
                            m = g8 * 8 + m8
                            tr(ptb[:, m8 * 128:(m8 + 1) * 128], att[rb][:, m * 128:(m + 1) * 128], identb[:],
                               [Batt[rb], cb("identb")], [psB[tb]], sig=(m8 == n8 - 1))
                        cp('dve', attT[rb][:, g8 * 8:g8 * 8 + n8, :].rearrange("p a b -> p (a b)"), ptb[:, 0:n8 * 128],
                           [psB[tb]], [BattT[rb]])
                        yield
                    for m in range(nk):
                        mm(ps[7][:, 0:128], vTM[hb][:, m, :], attT[rb][:, m, :], m == 0, m == nk - 1,
                           [BvTM[hb], BattT[rb]], [psB[7]])
                    stt(oacc[hb][:, qsl], ps[7][:, 0:128], 0.5, sz2[hb][:, qsl], ALU.mult, ALU.mult,
                        [psB[7], Bsz2[hb]], [Boacc[hb]])
                    yield

                def step(g_):
                    try:
                        next(g_)
                        return True
                    except StopIteration:
                        return False

                nheads = 8 if stage > 3 else 2
                pg = sb_proj(0)
                while step(pg):
                    pass
                for h in range(nheads):
                    pg = sb_proj(h + 1) if h + 1 < nheads else None
                    order = [15, 0, 14, 1, 13, 2, 12, 3, 11, 4, 10, 5, 9, 6, 8, 7]
                    active = []
                    free_sets = list(range(NRS))
                    nxt = 0
                    while nxt < NT or active:
                        while free_sets and nxt < NT:
                            s_ = free_sets.pop(0)
                            active.append((sb_row(h, order[nxt], s_), s_))
                            nxt += 1
                        for (g_, s_) in list(active):
                            if not step(g_):
                                active.remove((g_, s_))
                                free_sets.append(s_)
                        if pg is not None and not step(pg):
                            pg = None
                    while pg is not None and step(pg):
                        pass
                    P.dma('sp', d_oTsb[h], oacc[h % 2][:], reads=[Boacc[h % 2]], writes=[BoT_sb[h]])
                if stage == 3:
                    dump("oacc0", oacc[0][:], T, [Boacc[0]])
                    dump("oacc1", oacc[1][:], T, [Boacc[1]])
                barrier()

            if stage >= 4:
                BB = {}

                def B(name):
                    if name not in BB:
                        BB[name] = Buf(name)
                    return BB[name]

                QS = float(128 ** -0.5)
                TB2X = 24992 + 8 * 8192 + 17424 + 6144 + 3136 + 2048
                assert TB2X + 512 <= AR_BYTES
                halo = aview(0, [128, 72], F32)
                S = aview(320, [128, 8, 128], F32)
                Sbf = aview(4416, [128, 8, 128], BF16)
                vnE = aview(6464, [128, 8, 128], BF16)
                vnO = aview(8512, [128, 8, 128], BF16)
                t_b, t_a, t_G, t_eG, t_kd = [aview(10560 + 2048 * i_, [8, 512], F32) for i_ in range(5)]
                t_eGb = aview(10560 + 2048 * 3, [8, 512], BF16)
                t_kdb = aview(10560 + 2048 * 3 + 1024, [8, 512], BF16)
                tab = aview(20800, [128, 64], F32)
                ntabs = [aview(21056, [128, 64], F32), aview(TB2X, [128, 64], F32)]
                dec_bcs = [aview(21312, [128, 64], F32), aview(TB2X + 256, [128, 64], F32)]
                wbab = aview(21568, [128, 8, 16], BF16)
                nA = aview(21824, [8, 1], F32)
                decF = aview(21856, [8, 8], F32)
                ident4 = aview(21888, [128, 512], BF16)
                offd4 = aview(22912, [128, 512], BF16)
                maskb4 = aview(23936, [128, 512], BF16)
                mh1 = aview(24960, [128, 1], F32)
                dnngh = aview(24964, [128, 1], F32)
                OPB = 24992

                def slotv(slot):
                    b_ = OPB + slot * 8192
                    return dict(wpT=aview(b_, [128, 512], BF16), qdT=aview(b_ + 1024, [128, 512], BF16),
                                aT=aview(b_ + 2048, [128, 512], BF16), kdTM=aview(b_ + 3072, [128, 4, 128], BF16),
                                ub=aview(b_ + 4096, [128, 4, 128], F32), sz2=aview(b_ + 6144, [128, 512], BF16),
                                oacc=aview(b_ + 7168, [128, 512], BF16))
                TB = OPB + 8 * 8192
                xpre = aview(TB, [128, 516], F32)
                sqf = xpre
                accb = aview(TB + 2064, [128, 512], F32)
                rq = aview(TB + 4112, [128, 512], F32)
                kf = aview(TB + 6160, [128, 512], F32)
                kTn = aview(TB + 8208, [128, 512], BF16)
                qnT = aview(TB + 9232, [128, 512], BF16)
                vT2 = aview(TB + 10256, [128, 512], BF16)
                keT = aview(TB + 11280, [128, 512], BF16)
                kdT = aview(TB + 12304, [128, 512], BF16)
                vTM = aview(TB + 13328, [128, 4, 128], BF16)
                keTM = aview(TB + 14352, [128, 4, 128], BF16)
                Ebuf = aview(TB + 15376, [128, 512], F32)
                chb = [aview(TB + 17424 + 1024 * i_, [128, 512], BF16) for i_ in range(6)]
                TB2 = TB + 17424 + 6144
                o_tile = [aview(TB2 + 512 * i_, [128, 128], F32) for i_ in range(4)]
                onb = [aview(TB2 + 2048 + 256 * i_, [128, 128], BF16) for i_ in range(4)]
                ost = [aview(TB2 + 3072 + 16 * i_, [128, 4], F32) for i_ in range(4)]
                mhalf4 = aview(TB2 + 3136, [128, 512], F32)
                assert TB2 + 3136 + 2048 <= AR_BYTES
                XT0 = TB2X + 512
                Ebuf_s = [Ebuf, aview(XT0, [128, 512], F32)]
                vTM_s = [vTM, aview(XT0 + 2048, [128, 4, 128], BF16)]
                keTM_s = [keTM, aview(XT0 + 3072, [128, 4, 128], BF16)]
                assert XT0 + 4096 <= AR_BYTES
                psR6 = [Buf("ps6_%d" % i_) for i_ in range(4)]
                psR7 = [Buf("ps7_%d" % i_) for i_ in range(4)]
                psR2 = [Buf("ps2_%d" % i_) for i_ in range(4)]

                memset('pool', halo[:], 0.0, [B("halo")])
                memset('pool', S[:].rearrange("p a b -> p (a b)"), 0.0, [B("S%d" % h_) for h_ in range(8)])
                memset('pool', Sbf[:].rearrange("p a b -> p (a b)"), 0.0, [B("Sbf%d" % h_) for h_ in range(8)])
                memset('pool', vnE[:].rearrange("p a b -> p (a b)"), 0.0, [B("vnE%d" % h_) for h_ in range(8)])
                memset('pool', vnO[:].rearrange("p a b -> p (a b)"), 0.0, [B("vnO%d" % h_) for h_ in range(8)])
                memset('pool', mh1[:], -0.5, [B("mh1")])
                memset('pool', mhalf4[:], -0.5, [B("mh1")])
                for q_ in range(4):
                    cp('pool', ident4[:, q_ * 128:(q_ + 1) * 128], identb[:], [cb("identb")], [B("ident4")])
                    cp('pool', offd4[:, q_ * 128:(q_ + 1) * 128], offdb[:], [cb("offdb")], [B("offd4")])
                    cp('pool', maskb4[:, q_ * 128:(q_ + 1) * 128], maskbb[:], [cb("maskbb")], [B("maskb4")])
                ts('pool', dnngh[:], dnng[:], 0.5, None, ALU.mult, None, [cb("dnng")], [B("dnngh")])
                act(nA[:], alog[:], AF.Exp, [cb("alog")], [B("nA")])
                ts('pool', nA[:], nA[:], -1.0, None, ALU.mult, None, [B("nA")], [B("nA")])
                P.dma('pool', wbab[:].rearrange("p a b -> p (a b)"), d_wba, writes=[B("wbab")])
                pj = [0]

                def pbank():
                    pj[0] += 1
                    return pj[0] % 2

                def tables(tg):
                    tsl = slice(tg * 512, (tg + 1) * 512)
                    ntab, dec_bc = ntabs[tg % 2], dec_bcs[tg % 2]
                    Bnt, Bdc = B("ntab%d" % (tg % 2)), B("dec_bc%d" % (tg % 2))
                    for q_ in range(2):
                        for kc in range(NKC):
                            mm(ps[q_][0:8, 0:512], wbab[:, kc, q_ * 8:(q_ + 1) * 8], hT[:, kc, tsl], kc == 0, kc == NKC - 1,
                               [B("wbab"), BhT], [psB[q_]])
                    act(t_b[:], ps[0][0:8, 0:512], AF.Exp, [psB[0]], [B("t_b")], scale=-1.0)
                    ts('dve', t_b[:], t_b[:], 1.0, None, ALU.add, None, [B("t_b")], [B("t_b")])
                    P.op('dve', lambda e: e.reciprocal(t_b[:], t_b[:]), [B("t_b")], [B("t_b")])
                    act(t_a[:], ps[1][0:8, 0:512], AF.Exp, [psB[1], cb("dtb")], [B("t_a")], bias=dtb[:, 0:1])
                    act(t_a[:], t_a[:], AF.Ln, [B("t_a")], [B("t_a")], bias=1.0)
                    ts('dve', t_a[:], t_a[:], nA[:, 0:1], None, ALU.mult, None, [B("t_a"), B("nA")], [B("t_a")])
                    P.op('dve', lambda e: e.tensor_tensor_scan(t_G[:], rmask[:], t_a[:], 0.0, ALU.mult, ALU.add),
                         [B("t_a"), cb("rmask")], [B("t_G")])
                    act(t_eGb[:], t_G[:], AF.Exp, [B("t_G")], [B("t_eG")])
                    G3 = t_G[:].rearrange("p (n c) -> p n c", c=64)
                    tt('dve', t_kd[:].rearrange("p (n c) -> p n c", c=64), G3[:, :, 63:64].to_broadcast([8, 8, 64]), G3,
                       ALU.subtract, [B("t_G")], [B("t_kd")])
                    act(t_kdb[:], t_kd[:], AF.Exp, [B("t_kd")], [B("t_kdb")])
                    act(decF[:].rearrange("p (n c) -> p n c", c=1), G3[:, :, 63:64], AF.Exp, [B("t_G")], [B("decF")])
                    for tile in range(4):
                        for q_, src, bn in ((0, t_b, "t_b"), (1, t_G, "t_G")):
                            c0 = (tile * 2 + q_) * 8
                            mm(ps[2][:, c0:c0 + 8], src[:, tile * 128:(tile + 1) * 128], identf[0:8, 0:8], True, True,
                               [B(bn), cb("identf")], [psB[2]], sig=(tile == 3 and q_ == 1))
                    cp('dve', tab[:], ps[2][:, 0:64], [psB[2]], [B("tab")])
                    ts('pool', ntab[:], tab[:], -1.0, None, ALU.mult, None, [B("tab")], [Bnt])
                    for h_ in range(8):
                        mm(ps[2][:, 64 + h_ * 8:72 + h_ * 8], esel[:, h_ * 128:(h_ + 1) * 128], decF[:], True, True,
                           [cb("esel"), B("decF")], [psB[2]], sig=(h_ == 7))
                    cp('dve', dec_bc[:], ps[2][:, 64:128], [psB[2]], [Bdc])

                def silu2e(out, y, reads, writes):
                    i = sctr[0] % 2
                    sctr[0] += 1
                    tmp = silu_tmp[i][:, 0:512]
                    act(tmp, y, AF.Exp, reads, [Bsilu_tmp[i]], scale=-1.0)
                    act(tmp, tmp, AF.Ln, [Bsilu_tmp[i]], [Bsilu_tmp[i]], bias=1.0)
                    act(tmp, tmp, AF.Exp, [Bsilu_tmp[i]], [Bsilu_tmp[i]], scale=-1.0)
                    stt(out, y, 2.0, tmp, ALU.mult, ALU.mult, list(reads) + [Bsilu_tmp[i]], writes)

                def bcast(h, src, bn):
                    pi = pbank()
                    mm(ps[pi][:, 0:512], eselb[:, h * 128:(h + 1) * 128], src[:], True, True, [cb("eselb"), B(bn)], [psB[pi]])
                    return pi

                def proj(h, tg, slot):
                    ops = slotv(slot)
                    sn = "s%d_" % slot
                    for qi, nm in enumerate("qkv"):
                        cblk = qi * 8 + h
                        w, Bw = load_w(d_wmain[cblk])
                        pi = pbank()
                        proj_fm(w, Bw, tg, ps[pi], psB[pi])
                        cp('pool', xpre[:, 0:3], halo[:, cblk * 3:cblk * 3 + 3], [B("halo")], [B("xpre")])
                        act(xpre[:, 3:515], ps[pi][:, 0:512], AF.Copy, [psB[pi]], [B("xpre")])
                        cp('pool', halo[:, cblk * 3:cblk * 3 + 3], xpre[:, 512:515], [B("xpre")], [B("halo")])
                        yield
                        ts('dve', accb[:], xpre[:, 3:515], convw[:, cblk * 4 + 3:cblk * 4 + 4], None, ALU.mult, None,
                           [B("xpre"), cb("convw")], [B("acc")])
                        for j_ in range(3):
                            stt(accb[:], xpre[:, j_:j_ + 512], convw[:, cblk * 4 + j_:cblk * 4 + j_ + 1], accb[:], ALU.mult, ALU.add,
                                [B("xpre"), cb("convw"), B("acc")], [B("acc")])
                        yield
                        silu2e(accb[:], accb[:], [B("acc")], [B("acc")])
                        yield
                        if nm in "qk":
                            sqb = aview(TB, [128, 512], BF16)
                            act(sqb[:], accb[:], AF.Square, [B("acc")], [B("xpre")])
                            pi = pbank()
                            mm(ps[pi][:, 0:512], onesb[:], sqb[:], True, True, [cb("onesb"), B("xpre")], [psB[pi]])
                            act(rq[:], ps[pi][:, 0:512], AF.Ln, [psB[pi]], [B("rq")], bias=4.0 * EPS)
                            act(rq[:], rq[:], AF.Exp, [B("rq")], [B("rq")], scale=-0.5)
                            yield
                            if nm == "k":
                                tt('dve', kf[:], accb[:], rq[:], ALU.mult, [B("acc"), B("rq")], [B("kf")])
                                act(kTn[:], kf[:], AF.Copy, [B("kf")], [B("kTn")])
                                pi = bcast(h, t_eGb, "t_eG")
                                tt('dve', keT[:], kf[:], ps[pi][:, 0:512], ALU.mult, [B("kf"), psB[pi]], [B("keT")])
                                pi = bcast(h, t_kdb, "t_kdb")
                                tt('dve', kdT[:], kf[:], ps[pi][:, 0:512], ALU.mult, [B("kf"), psB[pi]], [B("kdT")])
                            else:
                                stt(kf[:], accb[:], QS, rq[:], ALU.mult, ALU.mult, [B("acc"), B("rq")], [B("kf")])
                                act(qnT[:], kf[:], AF.Copy, [B("kf")], [B("qnT")])
                                pi = bcast(h, t_eGb, "t_eG")
                                tt('dve', ops["qdT"][:], kf[:], ps[pi][:, 0:512], ALU.mult, [B("kf"), psB[pi]], [B(sn + "qdT")])
                        else:
                            act(vT2[:], accb[:], AF.Copy, [B("acc")], [B("vT2")])
                        yield
                    u_ = slot % 2
                    ntab_ = ntabs[tg % 2]
                    tcs_ = [slice(t_ * 128, (t_ + 1) * 128) for t_ in range(4)]
                    pi = pbank()
                    pb_ = ps[pi][:].bitcast(BF16)
                    for t_ in range(4):
                        tr(pb_[:, t_ * 128:(t_ + 1) * 128], vT2[:, tcs_[t_]], identb[:], [B("vT2"), cb("identb")], [psB[pi]], sig=False)
                    for t_ in range(4):
                        tr(pb_[:, 512 + t_ * 128:512 + (t_ + 1) * 128], keT[:, tcs_[t_]], identb[:], [B("keT"), cb("identb")], [psB[pi]],
                           sig=(t_ == 3))
                    act(vTM_s[u_][:].rearrange("p a b -> p (a b)"), pb_[:, 0:512], AF.Copy, [psB[pi]], [B("vTM%d" % u_)], scale=0.5)
                    act(keTM_s[u_][:].rearrange("p a b -> p (a b)"), pb_[:, 512:1024], AF.Copy, [psB[pi]], [B("keTM%d" % u_)])
                    yield
                    pi = pbank()
                    pb_ = ps[pi][:].bitcast(BF16)
                    for t_ in range(4):
                        tr(pb_[:, t_ * 128:(t_ + 1) * 128], kdT[:, tcs_[t_]], identb[:], [B("kdT"), cb("identb")], [psB[pi]], sig=(t_ == 3))
                    cp('dve', ops["kdTM"][:].rearrange("p a b -> p (a b)"), pb_[:, 0:512], [psB[pi]], [B(sn + "kdTM")])
                    yield

                def egen(h, tg, slot):
                    u_ = slot % 2
                    ntab_ = ntabs[tg % 2]
                    tcs_ = [slice(t_ * 128, (t_ + 1) * 128) for t_ in range(4)]
                    yield
                    pi = pbank()
                    mm(ps[pi][:, 0:512], esel[:, h * 128:(h + 1) * 128], t_G[:], True, False, [cb("esel"), B("t_G")], [psB[pi]], sig=False)
                    mm(ps[pi][:, 0:512], identb[:], maskb4[:], False, True, [cb("identb"), B("maskb4")], [psB[pi]])
                    for t_ in range(4):
                        act(Ebuf_s[u_][:, tcs_[t_]], ps[pi][:, tcs_[t_]], AF.Exp, [psB[pi], B("ntab%d" % (tg % 2))], [B("E%d" % u_)],
                            bias=ntab_[:, t_ * 16 + 8 + h:t_ * 16 + 9 + h])
                    yield
                    yield
                    ops = slotv(slot)
                    w, Bw = load_w(d_wmain[BLK_DNZ + h])
                    pi = pbank()
                    proj_fm(w, Bw, tg, ps[pi], psB[pi])
                    silu2e(ops["sz2"][:], ps[pi][:, 0:512], [psB[pi]], [B("s%d_sz2" % slot)])
                    yield

                def chain(h, tg, slot):
                    ops = slotv(slot)
                    ntab = ntabs[tg % 2]
                    Bnt = B("ntab%d" % (tg % 2))
                    sn = "s%d_" % slot
                    Nb, Mb, P0, P1, M2b, N2b = chb
                    BN, BM, BP0, BP1, BM2, BN2 = [B("ch%d" % i_) for i_ in range(6)]
                    tcs = [slice(t_ * 128, (t_ + 1) * 128) for t_ in range(4)]
                    for t_ in range(4):
                        mm(ps[3][:, tcs[t_]], kTn[:, tcs[t_]], kTn[:, tcs[t_]], True, True, [B("kTn")], [psB[3]], sig=(t_ == 3))
                    for t_ in range(4):
                        mm(ps[4][:, tcs[t_]], kTn[:, tcs[t_]], qnT[:, tcs[t_]], True, True, [B("kTn"), B("qnT")], [psB[4]], sig=(t_ == 3))
                    u_ = slot % 2
                    Ebuf, vTM, keTM = Ebuf_s[u_], vTM_s[u_], keTM_s[u_]
                    BE, BvTM_, BkeTM_ = B("E%d" % u_), B("vTM%d" % u_), B("keTM%d" % u_)
                    yield
                    for t_ in range(4):
                        stt(Nb[:, tcs[t_]], ps[3][:, tcs[t_]], tab[:, t_ * 16 + h:t_ * 16 + h + 1], Ebuf[:, tcs[t_]], ALU.mult, ALU.mult,
                            [psB[3], B("tab"), BE], [BN])
                    tt('dve', Nb[:], Nb[:], offd4[:], ALU.mult, [BN, B("offd4")], [BN])
                    yield
                    tt('dve', ops["aT"][:], ps[4][:, 0:512], Ebuf[:], ALU.mult, [psB[4], BE], [B(sn + "aT")])
                    yield
                    b5 = ps[5][:].bitcast(BF16)
                    for t_ in range(4):
                        tr(b5[:, t_ * 128:(t_ + 1) * 128], Nb[:, tcs[t_]], identb[:], [BN, cb("identb")], [psB[5]], sig=(t_ == 3))
                    yield
                    cp('dve', Mb[:], b5[:, 0:512], [psB[5]], [BM])
                    tt('dve', P0[:], ident4[:], Nb[:], ALU.subtract, [B("ident4"), BN], [BP0])
                    yield
                    cur, Bcur, oth, Both = P0, BP0, P1, BP1
                    for t_ in range(4):
                        mm(ps[3][:, tcs[t_]], Nb[:, tcs[t_]], Mb[:, tcs[t_]], True, True, [BN, BM], [psB[3]], sig=(t_ == 3))
                    for t_ in range(4):
                        mm(ps[4][:, tcs[t_]], Mb[:, tcs[t_]], Nb[:, tcs[t_]], True, True, [BN, BM], [psB[4]], sig=(t_ == 3))
                    yield
                    act(M2b[:], ps[3][:, 0:512], AF.Copy, [psB[3]], [BM2])
                    cp('dve', N2b[:], ps[4][:, 0:512], [psB[4]], [BN2])
                    Nb, N2b, BN, BN2 = N2b, Nb, BN2, BN
                    Mb, M2b, BM, BM2 = M2b, Mb, BM2, BM
                    yield
                    for lev in range(1, 6):
                        for t_ in range(4):
                            mm(ps[5][:, tcs[t_]], Mb[:, tcs[t_]], cur[:, tcs[t_]], True, True, [BM, Bcur], [psB[5]], sig=(t_ == 3))
                        if lev < 5:
                            for t_ in range(4):
                                mm(ps[3][:, tcs[t_]], Nb[:, tcs[t_]], Mb[:, tcs[t_]], True, True, [BN, BM], [psB[3]], sig=(t_ == 3))
                            if lev < 4:
                                for t_ in range(4):
                                    mm(ps[4][:, tcs[t_]], Mb[:, tcs[t_]], Nb[:, tcs[t_]], True, True, [BN, BM], [psB[4]], sig=(t_ == 3))
                        yield
                        tt('dve', oth[:], ps[5][:, 0:512], cur[:], ALU.add, [psB[5], Bcur], [Both])
                        cur, Bcur, oth, Both = oth, Both, cur, Bcur
                        if lev < 5:
                            act(M2b[:], ps[3][:, 0:512], AF.Copy, [psB[3]], [BM2])
                            if lev < 4:
                                cp('dve', N2b[:], ps[4][:, 0:512], [psB[4]], [BN2])
                            Nb, N2b, BN, BN2 = N2b, Nb, BN2, BN
                            Mb, M2b, BM, BM2 = M2b, Mb, BM2, BM
                        yield
                    for t_ in range(4):
                        mm(ps[3][:, tcs[t_]], cur[:, tcs[t_]], vTM[:, t_, :], True, True, [Bcur, BvTM_], [psB[3]], sig=(t_ == 3))
                    for t_ in range(4):
                        mm(ps[4][:, tcs[t_]], keTM[:, t_, :], cur[:, tcs[t_]], True, True, [Bcur, BkeTM_], [psB[4]], sig=(t_ == 3))
                    yield
                    for t_ in range(4):
                        ts('dve', ops["ub"][:, t_, :], ps[3][:, tcs[t_]], tab[:, t_ * 16 + h:t_ * 16 + h + 1], None, ALU.mult, None,
                           [psB[3], B("tab")], [B(sn + "ub")])
                    act(ops["wpT"][:], ps[4][:, 0:512], AF.Copy, [psB[4]], [B(sn + "wpT")])
                    yield

                def scan(h, tg, slot, hh):
                    ops = slotv(slot)
                    ntab, dec_bc = ntabs[tg % 2], dec_bcs[tg % 2]
                    Bnt, Bdc = B("ntab%d" % (tg % 2)), B("dec_bc%d" % (tg % 2))
                    sn = "s%d_" % slot
                    tsl0 = tg * 512
                    Pw = ps[6][:, hh * 128:(hh + 1) * 128]
                    Po = ps[7][:, hh * 128:(hh + 1) * 128]
                    BS, BSbf = B("S%d" % h), B("Sbf%d" % h)
                    for t_ in range(4):
                        tc = slice(t_ * 128, (t_ + 1) * 128)
                        for par in range(2):
                            ch = t_ * 2 + par
                            rows = slice(par * 64, par * 64 + 64)
                            vn = (vnE if par == 0 else vnO)[:, h, :]
                            Bvn = B(("vnE%d" if par == 0 else "vnO%d") % h)
                            mm(Pw, ops["wpT"][:, tc], Sbf[:, h, :], True, True, [B(sn + "wpT"), BSbf], [psB[6]])
                            yield
                            stt(vn[rows], Pw[rows], ntab[rows, t_ * 16 + h:t_ * 16 + h + 1], ops["ub"][rows, t_, :], ALU.mult, ALU.add,
                                [psB[6], Bnt, B(sn + "ub")], [Bvn])
                            yield
                            import os as _os
                            _ks = _os.environ.get("KSC", "PQS")
                            if "P" in _ks:
                                mm(Po, ops["qdT"][:, tc], Sbf[:, h, :], True, "Q" not in _ks, [B(sn + "qdT"), BSbf], [psB[7]], sig=("Q" not in _ks))
                            if "Q" in _ks:
                                mm(Po, ops["aT"][:, tc], vn, "P" not in _ks, True, [B(sn + "aT"), Bvn], [psB[7]])
                            if "S" in _ks:
                                mm(Pw, ops["kdTM"][:, t_, :], vn, True, True, [B(sn + "kdTM"), Bvn], [psB[6]])
                            yield
                            act(o_tile[hh][rows], Po[rows], AF.Copy, [psB[7]], [B("otile%d" % hh)])
                            stt(S[:, h, :], S[:, h, :], dec_bc[:, h * 8 + ch:h * 8 + ch + 1], Pw, ALU.mult, ALU.add,
                                [BS, Bdc, psB[6]], [BS])
                            yield
                            act(Sbf[:, h, :], S[:, h, :], AF.Copy, [BS], [BSbf])
                            yield
                        act(onb[hh][:], o_tile[hh][:], AF.Square, [B("otile%d" % hh)], [B("onb%d" % hh), B("ost%d" % hh)],
                            accum=ost[hh][:, 0:1])
                        ts('pool', ost[hh][:, 1:2], ost[hh][:, 0:1], 1.0 / 128, EPS, ALU.mult, ALU.add, [B("ost%d" % hh)], [B("ost%d" % hh)])
                        tt('pool', ost[hh][:, 2:3], ost[hh][:, 1:2], mh1[:], ALU.pow, [B("ost%d" % hh), B("mh1")], [B("ost%d" % hh)])
                        ts('dve', onb[hh][:], o_tile[hh][:], ost[hh][:, 2:3], None, ALU.mult, None,
                           [B("otile%d" % hh), B("ost%d" % hh)], [B("onb%d" % hh)])
                        p2b = ps[2][:].bitcast(BF16)[:, 256 + hh * 128:256 + (hh + 1) * 128]
                        tr(p2b, onb[hh][:], identb[:], [B("onb%d" % hh), cb("identb")], [psB[2]])
                        stt(ops["oacc"][:, tc], p2b, dnngh[:, 0:1], ops["sz2"][:, tc], ALU.mult, ALU.mult,
                            [psB[2], B("dnngh"), B(sn + "sz2")], [B(sn + "oacc")])
                        yield
                    P.dma('sp', d_oTdn[h][:, tsl0:tsl0 + 512], ops["oacc"][:], reads=[B(sn + "oacc")], writes=[BoT_dn[h]])

                def run_il(gens):
                    gens = list(gens)
                    while gens:
                        for g_ in list(gens):
                            try:
                                next(g_)
                            except StopIteration:
                                gens.remove(g_)

                if sub in (1, 2, 3) or 20 < sub < 60:
                    tables(0)
                    if sub == 1:
                        dump("tab", tab[:], 64, [B("tab")])
                        dump("dec_bc", dec_bcs[0][:], 64, [B("dec_bc0")])
                        dump("t_G", t_G[:], 512, [B("t_G")])
                        dump("t_b", t_b[:], 512, [B("t_b")])
                        raise StopBuild()
                    if 20 < sub < 30:
                        g_ = proj(0, 0, 0)
                        for _ in range(sub - 20):
                            next(g_)
                        dump("xpre", xpre[:, 0:512], 512, [B("xpre")])
                        dump("acc", accb[:], 512, [B("acc")])
                        dump("rq", rq[:], 512, [B("rq")])
                        raise StopBuild()
                    run_il([proj(0, 0, 0)])
                    if sub == 2:
                        dump("kTn", kTn[:], 512, [B("kTn")])
                        dump("qnT", qnT[:], 512, [B("qnT")])
                        dump("qdT", slotv(0)["qdT"][:], 512, [B("s0_qdT")])
                        dump("vT2", vT2[:], 512, [B("vT2")])
                        raise StopBuild()
                    if 30 < sub < 60:
                        g_ = chain(0, 0, 0)
                        for _ in range(sub - 30):
                            next(g_)
                        dump("E", Ebuf[:], 512, [B("E")])
                        dump("Nb", chb[0][:], 512, [B("ch0")])
                        dump("Mb", chb[1][:], 512, [B("ch1")])
                        dump("P0", chb[2][:], 512, [B("ch2")])
                        raise StopBuild()
                    run_il([chain(0, 0, 0)])
                    dump("ub", slotv(0)["ub"][:].rearrange("p a b -> p (a b)"), 512, [B("s0_ub")])
                    dump("wpT", slotv(0)["wpT"][:], 512, [B("s0_wpT")])
                    dump("aT", slotv(0)["aT"][:], 512, [B("s0_aT")])
                    raise StopBuild()
                if 100 <= sub < 400:
                    tables(0)
                    for h_ in range(4):
                        run_il([proj(h_, 0, h_)])
                        run_il([chain(h_, 0, h_)])
                    nh = 1 if sub < 200 else (4 if sub < 300 else 2)
                    gs = [scan(h_, 0, h_, h_) for h_ in range(nh)]
                    for _ in range(sub % 100):
                        for g_ in gs:
                            next(g_)
                    dump("S0", S[:, 0, :], 128, [B("S0")])
                    dump("vnE0", vnE[:, 0, :], 128, [B("vnE0")])
                    dump("vnO0", vnO[:, 0, :], 128, [B("vnO0")])
                    dump("otile0", o_tile[0][:], 128, [B("otile0")])
                    raise StopBuild()
                units = [(tg, h) for tg in range(NTG if stage > 4 else 1) for h in range(8)]
                nu = len(units)
                pending_scans = []
                scan_active = []
                scan_group = [None]

                def stepg(lst):
                    for g_ in list(lst):
                        try:
                            next(g_)
                        except StopIteration:
                            lst.remove(g_)

                for i_ in range(nu + 1):
                    gens = []
                    if i_ >= 1:
                        tg_, h_ = units[i_ - 1]
                        slot_ = (((i_ - 1) // 4) % 2) * 4 + (h_ % 4)
                        cg = chain(h_, tg_, slot_)
                        if i_ < nu and units[i_][1] == 0:
                            lst = [cg]
                            while lst:
                                stepg(lst)
                                stepg(scan_active)
                        else:
                            gens.append(cg)
                    if scan_active and 4 * (scan_group[0] + 2) <= i_:
                        while scan_active:
                            stepg(scan_active)
                    if i_ < nu:
                        tg, h = units[i_]
                        if h == 0:
                            tables(tg)
                        slot = ((i_ // 4) % 2) * 4 + (h % 4)
                        gens.append(proj(h, tg, slot))
                        gens.append(egen(h, tg, slot))
                    if i_ >= 5 and (i_ - 5) % 4 == 0:
                        while scan_active:
                            stepg(scan_active)
                        g0 = (i_ - 5) // 4
                        scan_group[0] = g0
                        for hh in range(4):
                            tg_, h_ = units[g0 * 4 + hh]
                            scan_active.append(scan(h_, tg_, (g0 % 2) * 4 + hh, hh))
                    while gens:
                        stepg(gens)
                        stepg(scan_active)
                while scan_active:
                    stepg(scan_active)
                ngroups = nu // 4
                done = max(0, (nu + 1 - 5 + 3) // 4) if nu + 1 > 5 else 0
                done = len([i_ for i_ in range(nu + 1) if i_ >= 5 and (i_ - 5) % 4 == 0])
                for g0 in range(done, ngroups):
                    run_il([scan(units[g0 * 4 + hh][1], units[g0 * 4 + hh][0], (g0 % 2) * 4 + hh, hh) for hh in range(4)])
                if stage == 4:
                    for s_ in range(2):
                        dump("oacc%d" % s_, slotv(s_)["oacc"][:], 512, [B("s%d_oacc" % s_)])
                barrier()

            if stage >= 5:
                woutb = aview(0, [128, NKC, 1024], BF16)
                oTsb2 = aview(16384, [128, 16, 512], BF16)
                oTdn2 = aview(32768, [128, 16, 512], BF16)
                mergedT = [aview(49152 + 8192 * t_, [128, 8, 512], BF16) for t_ in range(4)]
                sgt = [[aview(81920 + (3 * r_ + k_) * 2048, [128, 512], F32) for k_ in range(3)] for r_ in range(2)]
                a12 = [aview(94208, [128, 512], F32), aview(96256, [128, 512], F32)]
                xt = [aview(98304, [128, D], F32), aview(102400, [128, D], F32)]
                rt = [aview(106496, [128, D], F32), aview(110592, [128, D], F32)]
                fst = [aview(114688, [128, 4], F32), aview(114704, [128, 4], F32)]
                mhalf = aview(114720, [128, 1], F32)
                Bwoutb, BoTsb2, BoTdn2 = Buf("woutb"), Buf("oTsb2"), Buf("oTdn2")
                BmergedT = [Buf("mergedT%d" % t_) for t_ in range(4)]
                Bsgt = [[Buf("sgt%d%d" % (r_, k_)) for k_ in range(3)] for r_ in range(2)]
                Ba12 = [Buf("a120"), Buf("a121")]
                Bxt = [Buf("xt0"), Buf("xt1")]
                Brt = [Buf("rt0"), Buf("rt1")]
                Bfst = [Buf("fst0"), Buf("fst1")]
                Bmhalf = Buf("mhalf")
                memset('pool', mhalf[:], -0.5, [Bmhalf])
                for kc in range(NKC):
                    P.dma('pool', woutb[:, kc, :], d_wout[:, kc * 1024:(kc + 1) * 1024], writes=[Bwoutb])
                ctr5 = {'g': 0, 'f': 0}

                def merge_pair(p):
                    tgs = (2 * p, 2 * p + 1)
                    for ti, tg in enumerate(tgs):
                        tsl = slice(tg * 512, (tg + 1) * 512)
                        P.dma('sp', oTsb2[:, ti * 8:(ti + 1) * 8, :], d_oTsb[:, :, tsl].rearrange("h p t -> p h t"),
                              reads=[BoT_sb[h_] for h_ in range(8)], writes=[BoTsb2])
                        P.dma('sp', oTdn2[:, ti * 8:(ti + 1) * 8, :], d_oTdn[:, :, tsl].rearrange("h p t -> p h t"),
                              reads=[BoT_dn[h_] for h_ in range(8)], writes=[BoTdn2])
                    for f in range(8):
                        wg = [load_w(d_wmain[blk + f]) for blk in (BLK_GDN, BLK_GSB, BLK_GM)]
                        wdn, Bwdn = load_w(d_wbrdn[f])
                        wsb, Bwsb = load_w(d_wbrsb[f])
                        bi = wctr[1] % NWB
                        wctr[1] += 1
                        P.dma('pool', wbf[bi][:, 0:2, :].rearrange("p a b -> p (a b)"), d_wbrm[f], writes=[Bwbf[bi]])
                        for ti, tg in enumerate(tgs):
                            tsl = slice(tg * 512, (tg + 1) * 512)
                            r_ = ctr5['g'] % 2
                            ctr5['g'] += 1
                            for k_ in range(3):
                                w, Bw = wg[k_]
                                proj_fm(w, Bw, tg, ps[k_], psB[k_])
                                act(sgt[r_][k_][:], ps[k_][:, 0:512], AF.Tanh, [psB[k_]], [Bsgt[r_][k_]], scale=0.5)
                                yield
                            for kc in range(NKC):
                                mm(ps[3][:, 0:512], wdn[:, kc, :], oTdn2[:, ti * 8 + kc, :], kc == 0, kc == NKC - 1, [Bwdn, BoTdn2], [psB[3]])
                            stt(sgt[r_][0][:], sgt[r_][0][:], 1.0, ps[3][:, 0:512], ALU.add, ALU.mult,
                                [Bsgt[r_][0], psB[3]], [Bsgt[r_][0]])
                            yield
                            for kc in range(NKC):
                                mm(ps[4][:, 0:512], wsb[:, kc, :], oTsb2[:, ti * 8 + kc, :], kc == 0, kc == NKC - 1, [Bwsb, BoTsb2], [psB[4]])
                            stt(sgt[r_][1][:], sgt[r_][1][:], 1.0, ps[4][:, 0:512], ALU.add, ALU.mult,
                                [Bsgt[r_][1], psB[4]], [Bsgt[r_][1]])
                            yield
                            for kc in range(2):
                                mm(ps[5][:, 0:512], wbf[bi][:, kc, :], oT_m[:, kc, tsl], kc == 0, kc == 1, [Bwbf[bi], BoT_m], [psB[5]])
                            stt(sgt[r_][2][:], sgt[r_][2][:], 1.0, ps[5][:, 0:512], ALU.add, ALU.mult,
                                [Bsgt[r_][2], psB[5]], [Bsgt[r_][2]])
                            tt('dve', a12[r_][:], sgt[r_][0][:], sgt[r_][1][:], ALU.add, [Bsgt[r_][0], Bsgt[r_][1]], [Ba12[r_]])
                            tt('dve', mergedT[tg][:, f, :], a12[r_][:], sgt[r_][2][:], ALU.add, [Ba12[r_], Bsgt[r_][2]], [BmergedT[tg]])
                            yield

                def final_gen(tg):
                    for jj in range(4):
                        j = tg * 4 + jj
                        fb = ctr5['f'] % 2
                        ctr5['f'] += 1
                        P.dma('sp', xt[fb][:], d_x[j * 128:(j + 1) * 128, :], writes=[Bxt[fb]])
                        for half in range(2):
                            for kc in range(NKC):
                                mm(ps[6 + half][:, 0:512], mergedT[tg][:, kc, jj * 128:(jj + 1) * 128],
                                   woutb[:, kc, half * 512:(half + 1) * 512], kc == 0, kc == NKC - 1,
                                   [BmergedT[tg], Bwoutb], [psB[6 + half]])
                            stt(rt[fb][:, half * 512:(half + 1) * 512], ps[6 + half][:, 0:512], 0.5, xt[fb][:, half * 512:(half + 1) * 512],
                                ALU.mult, ALU.add, [psB[6 + half], Bxt[fb]], [Brt[fb]])
                            yield
                        act(xt[fb][:], rt[fb][:], AF.Square, [Brt[fb], Bxt[fb]], [Bxt[fb], Bfst[fb]], accum=fst[fb][:, 0:1])
                        yield
                        ts('pool', fst[fb][:, 1:2], fst[fb][:, 0:1], 1.0 / D, EPS, ALU.mult, ALU.add, [Bfst[fb]], [Bfst[fb]])
                        tt('pool', fst[fb][:, 2:3], fst[fb][:, 1:2], mhalf[:], ALU.pow, [Bfst[fb], Bmhalf], [Bfst[fb]])
                        yield
                        stt(rt[fb][:], rt[fb][:], fst[fb][:, 2:3], fing[:], ALU.mult, ALU.mult, [Brt[fb], Bfst[fb], cb("fing")], [Brt[fb]])
                        P.dma('sp', d_out[j * 128:(j + 1) * 128, :], rt[fb][:], reads=[Brt[fb]], final=True)
                        yield

                def finals(tgs_):
                    for tg_ in tgs_:
                        for _ in final_gen(tg_):
                            yield

                def run_il5(gens):
                    gens = list(gens)
                    while gens:
                        for g_ in list(gens):
                            try:
                                next(g_)
                            except StopIteration:
                                gens.remove(g_)

                run_il5([merge_pair(0)])
                run_il5([merge_pair(1), finals((0, 1))])
                run_il5([finals((2, 3))])

        except StopBuild:
            pass
        barrier()
        P.emit()
    nc._dbg_items = dbg_items
    return nc


def _consts():
    idx = np.arange(128)
    s_, c_ = idx[:, None], idx[None, :]
    same = (s_ // 64) == (c_ // 64)
    maskb = np.where(same & (c_ >= s_), 0.0, NEGBIG).astype(np.float32)
    offd = (s_ != c_).astype(np.float32)
    causal = (c_ < s_).astype(np.float32)
    esel = np.zeros((8, 8, 128), np.float32)
    for h in range(8):
        esel[h, h, :] = 1.0
    rmask = np.ones((8, 512), np.float32)
    rmask[:, ::64] = 0.0
    return dict(ident=np.eye(128, dtype=np.float32), maskb=maskb, offd=offd, causal=causal,
                esel=esel.reshape(8, 1024), rmask=rmask)


def _layout_shared(inp):
    f = np.float32
    w_in = np.asarray(inp["w_in"][0], f)
    def blocks(w):
        n = w.shape[1] // 128
        return np.ascontiguousarray(w.reshape(8, 128, n, 128).transpose(2, 1, 0, 3).reshape(n, 128, 1024))
    main_cols = np.concatenate([np.arange(0, 4096), np.arange(4112, 11792)])
    wmain = blocks(w_in[:, main_cols])
    wba = np.ascontiguousarray(w_in[:, 4096:4112].reshape(8, 128, 16).transpose(1, 0, 2).reshape(128, 128))
    wmkv = np.ascontiguousarray(np.asarray(inp["w_mem_kv"][0], f).reshape(8, 128, 512).transpose(1, 0, 2).reshape(128, 4096))
    wbrdn = blocks(np.asarray(inp["w_br_dn"][0], f))
    wbrsb = blocks(np.asarray(inp["w_br_sb"][0], f))
    wm = np.asarray(inp["w_br_mem"][0], f)
    wbrm = np.ascontiguousarray(wm.reshape(2, 128, 8, 128).transpose(2, 1, 0, 3).reshape(8, 128, 256))
    wout = np.ascontiguousarray(np.asarray(inp["w_out"][0], f).reshape(8, 128, 1024).transpose(1, 0, 2).reshape(128, 8192))
    convw = np.ascontiguousarray(np.asarray(inp["conv_w"][0], f).reshape(4, 24, 128).transpose(2, 1, 0).reshape(128, 96))
    d = dict(wmain=wmain, wba=wba, wmkv=wmkv, wbrdn=wbrdn, wbrsb=wbrsb, wbrm=wbrm, wout=wout, convw=convw,
             normg=np.ascontiguousarray(np.asarray(inp["norm_g"][0], f).reshape(8, 128).T),
             memg=np.ascontiguousarray(np.asarray(inp["mem_norm_g"][0], f).reshape(8, 128).T),
             fing=np.asarray(inp["final_g"], f).reshape(1, 1024),
             dnng=np.asarray(inp["dn_norm_g"][0], f).reshape(128, 1),
             alog=np.asarray(inp["a_log"][0], f).reshape(8, 1),
             dtb=np.asarray(inp["dt_bias"][0], f).reshape(8, 1))
    d.update(_consts())
    return d


def _layout_core(inp, b):
    f = np.float32
    x = np.asarray(inp["x"][b], f)
    mem = np.asarray(inp["mem"][b], f)
    xT = np.ascontiguousarray(x.T.reshape(8, 128, 2048).transpose(1, 0, 2).reshape(128, 8 * 2048))
    memT = np.ascontiguousarray(mem.T.reshape(8, 128, 256).transpose(1, 0, 2).reshape(128, 8 * 256))
    return dict(xT=xT, x=np.ascontiguousarray(x), memT=memT)


def kernel(**inputs):
    shared = _layout_shared(inputs)
    in_maps = []
    for b in range(8):
        m = dict(shared)
        m.update(_layout_core(inputs, b))
        in_maps.append(m)
    nc = build_nc()
    res = run_bass_kernel_spmd(nc, in_maps, core_ids=list(range(8)))
    out = np.stack([np.asarray(res.results[b]["out"], np.float32) for b in range(8)], axis=0)
    return out
```

```python
import numpy as np
import concourse.bass as bass
import concourse.mybir as mybir
from concourse.bass_utils import run_bass_kernel_spmd
from contextlib import ExitStack

F32 = mybir.dt.float32
BF16 = mybir.dt.bfloat16
AF = mybir.ActivationFunctionType
ALU = mybir.AluOpType
AX = mybir.AxisListType

ENGS = ('pe', 'act', 'dve', 'pool', 'sp')


class Buf:
    __slots__ = ('name', 'w', 'r', 'const')

    def __init__(self, name, const=False):
        self.name = name
        self.w = None
        self.r = []
        self.const = const


class Prog:
    def __init__(self, nc, es, ndma=48):
        self.nc = nc
        self.streams = {e: [] for e in ENGS}
        self.esem = {e: es.enter_context(nc.semaphore("sem_" + e)) for e in ENGS}
        self.dsem = [es.enter_context(nc.semaphore("dsem%d" % i)) for i in range(ndma)]
        self.dcnt = [0] * ndma
        self.dnext = 0
        self.final = []
        self.nops = 0

    def _deps(self, eng, reads, writes):
        need = {}

        def add(tok):
            if tok is None:
                return
            k, v = tok
            if eng == 'pe' and k == 'pe':
                return
            if need.get(k, -1) < v:
                need[k] = v
        for b in reads:
            add(b.w)
        for b in writes:
            add(b.w)
            for t in b.r:
                add(t)
        return list(need.items())

    def op(self, eng, fn, reads=(), writes=(), sig=True):
        deps = self._deps(eng, reads, writes)
        idx = len(self.streams[eng])
        tok = (eng, idx)
        for b in writes:
            b.w = tok
            b.r = []
        for b in reads:
            if not b.const:
                b.r.append(tok)
        self.streams[eng].append(dict(fn=fn, deps=deps, sig=sig, dma=None))
        self.nops += 1

    def dma(self, eng, out, in_, reads=(), writes=(), final=False, **kw):
        deps = self._deps(eng, reads, writes)
        idx = self.dnext
        self.dnext = (self.dnext + 1) % len(self.dsem)
        key = ('d', idx)
        prev = self.dcnt[idx]
        if prev > 0:
            deps.append((key, prev))
        self.dcnt[idx] += 16
        tok = (key, self.dcnt[idx])
        for b in writes:
            b.w = tok
            b.r = []
        for b in reads:
            if not b.const:
                b.r.append(tok)
        fn = lambda e, out=out, in_=in_, kw=kw: e.dma_start(out=out, in_=in_, **kw)
        self.streams[eng].append(dict(fn=fn, deps=deps, sig=False, dma=idx))
        if final:
            self.final.append(tok)
        self.nops += 1

    def barrier(self):
        toks = [(e, len(self.streams[e]) - 1) for e in ENGS if self._last_sig(e) is not None]
        toks = [(e, self._last_sig(e)) for e, _ in toks]
        toks += [(('d', i), self.dcnt[i]) for i in range(len(self.dsem)) if self.dcnt[i] > 0]
        for e in ENGS:
            self.streams[e].append(dict(fn=None, deps=[t for t in toks if t[0] != e], sig=False, dma=None))

    def _last_sig(self, e):
        st = self.streams[e]
        for i in range(len(st) - 1, -1, -1):
            if st[i]['fn'] is not None and st[i]['dma'] is None and st[i]['sig']:
                return i
        return None

    def emit(self):
        nc = self.nc
        self.streams['sp'].append(dict(fn=None, deps=list(self.final), sig=False, dma=None))
        nxt = {}
        for e in ENGS:
            st = self.streams[e]
            r = [None] * len(st)
            cur = None
            for i in range(len(st) - 1, -1, -1):
                if st[i]['fn'] is not None and st[i]['dma'] is None and st[i]['sig']:
                    cur = i
                r[i] = cur
            nxt[e] = r
        needed = {e: set() for e in ENGS}
        for e in ENGS:
            for ent in self.streams[e]:
                for k, v in ent['deps']:
                    if not isinstance(k, tuple):
                        t = nxt[k][v]
                        assert t is not None, "dependency on an op that is never signalled"
                        needed[k].add(t)
        count = {}
        for e in ENGS:
            for rank, i in enumerate(sorted(needed[e])):
                count[(e, i)] = rank + 1
        self.n_incs = sum(len(v) for v in needed.values())
        with nc.Block() as block:
            def run(name):
                def f(eng):
                    waited = {}
                    for i, ent in enumerate(self.streams[name]):
                        for k, v in ent['deps']:
                            if isinstance(k, tuple):
                                sem, val = self.dsem[k[1]], v
                            else:
                                sem, val = self.esem[k], count[(k, nxt[k][v])]
                            if waited.get(k, 0) < val:
                                waited[k] = val
                                eng.wait_ge(sem, val)
                        if ent['fn'] is None:
                            continue
                        inst = ent['fn'](eng)
                        if ent['dma'] is not None:
                            inst.then_inc(self.dsem[ent['dma']], 16)
                        elif i in needed[name]:
                            inst.then_inc(self.esem[name], 1)
                return f
            block.tensor(run('pe'))
            block.scalar(run('act'))
            block.vector(run('dve'))
            block.gpsimd(run('pool'))
            block.sync(run('sp'))


T = 2048
D = 1024
NKC = 8
NT = 16
NTG = 4
NEGBIG = -30000.0
EPS = 1e-6

BLK_DNQ, BLK_DNK, BLK_DNV, BLK_DNZ = 0, 8, 16, 24
BLK_SBQ, BLK_SBK, BLK_SBV, BLK_SBZ = 32, 40, 48, 56
BLK_MQ, BLK_MZ = 64, 66
BLK_GDN, BLK_GSB, BLK_GM = 68, 76, 84
NBLK = 92


class StopBuild(Exception):
    pass


def build_nc(stage=99, dbg_n=0, sub=0, skip=()):
    nc = bass.Bass("TRN2", target_bir_lowering=False)

    def din(name, shape):
        return nc.dram_tensor(name, list(shape), F32, kind="ExternalInput").ap()

    d_xT = din("xT", [128, NKC * T])
    d_x = din("x", [T, D])
    d_memT = din("memT", [128, NKC * 256])
    d_wmain = din("wmain", [NBLK, 128, 1024])
    d_wba = din("wba", [128, NKC * 16])
    d_wmkv = din("wmkv", [128, NKC * 512])
    d_wbrdn = din("wbrdn", [8, 128, 1024])
    d_wbrsb = din("wbrsb", [8, 128, 1024])
    d_wbrm = din("wbrm", [8, 128, 256])
    d_wout = din("wout", [128, NKC * 1024])
    d_convw = din("convw", [128, 96])
    d_normg = din("normg", [128, 8])
    d_memg = din("memg", [128, 8])
    d_fing = din("fing", [1, D])
    d_dnng = din("dnng", [128, 1])
    d_alog = din("alog", [8, 1])
    d_dtb = din("dtb", [8, 1])
    d_ident = din("ident", [128, 128])
    d_maskb = din("maskb", [128, 128])
    d_offd = din("offd", [128, 128])
    d_causal = din("causal", [128, 128])
    d_esel = din("esel", [8, 8 * 128])
    d_rmask = din("rmask", [8, 512])
    d_out = nc.dram_tensor("out", [T, D], F32, kind="ExternalOutput").ap()
    d_dbg = None
    if dbg_n:
        d_dbg = nc.dram_tensor("dbg", [128, dbg_n], F32, kind="ExternalOutput").ap()

    with ExitStack() as es:
        P = Prog(nc, es, ndma=48)

        def sb(name, shape, dt):
            return es.enter_context(nc.sbuf_tensor("s_" + name, list(shape), dt))

        def mm(out, lhsT, rhs, start, stop, reads, writes, sig=None):
            if sig is None:
                sig = stop
            P.op('pe', lambda e: e.matmul(out, lhsT, rhs, start=start, stop=stop), reads, writes, sig=sig)

        def tr(out, in_, ident, reads, writes, sig=True):
            P.op('pe', lambda e: e.transpose(out, in_, ident), reads, writes, sig=sig)

        def act(out, in_, func, reads, writes, scale=None, bias=None, accum=None):
            kw = {}
            if scale is not None:
                kw['scale'] = scale
            if bias is not None:
                kw['bias'] = bias
            if accum is not None:
                kw['accum_out'] = accum
            P.op('act', lambda e: e.activation(out, in_, func, **kw), reads, writes)

        def tt(eng, out, a, b, op, reads, writes):
            P.op(eng, lambda e: e.tensor_tensor(out, a, b, op), reads, writes)

        def ts(eng, out, a, s1, s2, op0, op1, reads, writes):
            if op1 is None:
                P.op(eng, lambda e: e.tensor_scalar(out, a, s1, None, op0), reads, writes)
            else:
                P.op(eng, lambda e: e.tensor_scalar(out, a, s1, s2, op0, op1), reads, writes)

        def stt(out, a, scalar, b, op0, op1, reads, writes):
            P.op('dve', lambda e: e.scalar_tensor_tensor(out, a, scalar, b, op0, op1), reads, writes)

        def cp(eng, out, in_, reads, writes):
            P.op(eng, lambda e: e.tensor_copy(out, in_), reads, writes)

        def memset(eng, ap, val, writes):
            P.op(eng, lambda e: e.memset(ap, val), (), writes)

        def barrier():
            P.barrier()

        ps = [es.enter_context(nc.psum_tensor("ps%d" % i, [128, 512], F32)) for i in range(8)]
        psB = [Buf("ps%d" % i) for i in range(8)]

        hT = sb("hT", [128, NKC, T], BF16)
        BhT = Buf("hT")
        d_oTdn = nc.dram_tensor("oTdn_scr", [8, 128, T], BF16).ap()
        d_oTsb = nc.dram_tensor("oTsb_scr", [8, 128, T], BF16).ap()
        oT_m = sb("oT_m", [128, 2, T], BF16)
        BoT_dn = [Buf("oTdn%d" % h) for h in range(8)]
        BoT_sb = [Buf("oTsb%d" % h) for h in range(8)]
        BoT_m = Buf("oTm")
        identf = sb("identf", [128, 128], F32)
        identb = sb("identb", [128, 128], BF16)
        onesf = sb("onesf", [128, 128], F32)
        onesb = sb("onesb", [128, 128], BF16)
        eselb = sb("eselb", [8, 8 * 128], BF16)
        maskbf = sb("maskbf", [128, 128], F32)
        maskbb = sb("maskbb", [128, 128], BF16)
        offdf = sb("offdf", [128, 128], F32)
        offdb = sb("offdb", [128, 128], BF16)
        causf = sb("causf", [128, 128], F32)
        causb = sb("causb", [128, 128], BF16)
        esel = sb("esel", [8, 8 * 128], F32)
        rmask = sb("rmask", [8, 512], F32)
        convw = sb("convw", [128, 96], F32)
        normg = sb("normg", [128, 8], F32)
        memg = sb("memg", [128, 8], F32)
        fing = sb("fing", [128, D], F32)
        dnng = sb("dnng", [128, 1], F32)
        alog = sb("alog", [8, 1], F32)
        dtb = sb("dtb", [8, 1], F32)
        NWB = 11
        wbf = [sb("wbf%d" % i, [128, NKC, 128], BF16) for i in range(NWB)]
        Bwbf = [Buf("wbf%d" % i) for i in range(NWB)]
        wctr = [0, 0]
        AR_BYTES = 122 * 1024
        arena = sb("arena", [128, AR_BYTES // 4], F32)

        def aview(off, shape, dt):
            n = 1
            for s_ in shape[1:]:
                n *= s_
            esz = 4 if dt == F32 else 2
            assert off % 4 == 0 and (n * esz) % 4 == 0
            assert off + n * esz <= AR_BYTES, (off, shape)
            v = arena[:, off // 4: off // 4 + (n * esz) // 4]
            if dt != F32:
                v = v.bitcast(dt)
            if len(shape) == 3:
                v = v.rearrange("p (a b) -> p a b", a=shape[1])
            return v[0:shape[0]] if shape[0] != 128 else v

        dbg_items = []
        dbg_off = [0]
        dbgst = sb("dbgst", [128, 512], F32) if dbg_n else None
        Bdbg = Buf("dbgst")

        def dump(name, ap, n, reads):
            rows = ap.shape[0]
            for c0 in range(0, n, 512):
                cn = min(512, n - c0)
                cp('dve', dbgst[0:rows, 0:cn], ap[:, c0:c0 + cn], list(reads), [Bdbg])
                P.dma('sp', d_dbg[0:rows, dbg_off[0] + c0:dbg_off[0] + c0 + cn], dbgst[0:rows, 0:cn], reads=[Bdbg], final=True)
            dbg_items.append((name, dbg_off[0], rows, n))
            dbg_off[0] += n
            assert dbg_off[0] <= dbg_n

        def load_w(src):
            bi = wctr[1] % NWB
            wctr[1] += 1
            P.dma('pool', wbf[bi][:].rearrange("p a b -> p (a b)"), src, writes=[Bwbf[bi]])
            return wbf[bi], Bwbf[bi]

        def proj_fm(w, Bw, tg, pst, Bps, ncols=512, t0=None):
            if t0 is None:
                t0 = tg * 512
            for kc in range(NKC):
                mm(pst[:, 0:ncols], w[:, kc, :], hT[:, kc, t0:t0 + ncols], kc == 0, kc == NKC - 1,
                   [Bw, BhT], [Bps])

        silu_tmp = [sb("silutmp%d" % i, [128, 512], F32) for i in range(2)]
        Bsilu_tmp = [Buf("silutmp%d" % i) for i in range(2)]
        sctr = [0]

        def silu2(out, y, ncols, reads, writes):
            i = sctr[0] % 2
            sctr[0] += 1
            act(silu_tmp[i][:, 0:ncols], y, AF.Tanh, reads, [Bsilu_tmp[i]], scale=0.5)
            stt(out, silu_tmp[i][:, 0:ncols], 1.0, y, ALU.add, ALU.mult, list(reads) + [Bsilu_tmp[i]], writes)

        CB = {}

        def cb(name):
            if name not in CB:
                CB[name] = Buf(name, const=True)
            return CB[name]

        for (nm, t_, d_) in [("identf", identf, d_ident), ("maskbf", maskbf, d_maskb), ("offdf", offdf, d_offd),
                             ("causf", causf, d_causal), ("esel", esel, d_esel), ("rmask", rmask, d_rmask),
                             ("convw", convw, d_convw), ("normg", normg, d_normg), ("memg", memg, d_memg),
                             ("dnng", dnng, d_dnng), ("alog", alog, d_alog), ("dtb", dtb, d_dtb)]:
            P.dma('sp', t_[:], d_, writes=[cb(nm)])
        P.dma('sp', fing[:], d_fing[0:1, :].partition_broadcast(128), writes=[cb("fing")])
        cp('pool', identb[:], identf[:], [cb("identf")], [cb("identb")])
        cp('pool', maskbb[:], maskbf[:], [cb("maskbf")], [cb("maskbb")])
        cp('pool', offdb[:], offdf[:], [cb("offdf")], [cb("offdb")])
        cp('pool', causb[:], causf[:], [cb("causf")], [cb("causb")])
        memset('pool', onesf[:], 1.0, [cb("onesf")])
        memset('pool', onesb[:], 1.0, [cb("onesb")])
        cp('pool', eselb[:], esel[:], [cb("esel")], [cb("eselb")])

        try:

            if stage == 0:
                dump('identb', identb[:], 128, [cb('identb')])
                dump('fing', fing[:], 1024, [cb('fing')])
            if stage >= 1:
                xc = [aview(0, [128, T], F32), aview(8192, [128, T], F32)]
                sq = [aview(16384, [128, T], BF16), aview(24576, [128, T], BF16)]
                rstd_bc = aview(32768, [128, T], F32)
                Bxc = [Buf("xc0"), Buf("xc1")]
                Bsq = [Buf("sq0"), Buf("sq1")]
                Brstd = Buf("rstd_bc")
                for kc in range(NKC):
                    i = kc % 2
                    P.dma('sp', xc[i][:], d_xT[:, kc * T:(kc + 1) * T], writes=[Bxc[i]])
                    act(sq[i][:], xc[i][:], AF.Square, [Bxc[i]], [Bsq[i]])
                    for tg in range(NTG):
                        mm(ps[tg][:, 0:512], onesb[:], sq[i][:, tg * 512:(tg + 1) * 512], kc == 0, kc == NKC - 1,
                           [cb("onesb"), Bsq[i]], [psB[tg]], sig=True)
                for tg in range(NTG):
                    act(rstd_bc[:, tg * 512:(tg + 1) * 512], ps[tg][:, 0:512], AF.Ln, [psB[tg]], [Brstd],
                        scale=1.0 / D, bias=EPS)
                act(rstd_bc[:], rstd_bc[:], AF.Exp, [Brstd], [Brstd], scale=-0.5)
                for kc in range(NKC):
                    i = kc % 2
                    P.dma('sp', xc[i][:], d_xT[:, kc * T:(kc + 1) * T], writes=[Bxc[i]])
                    stt(hT[:, kc, :], xc[i][:], normg[:, kc:kc + 1], rstd_bc[:], ALU.mult, ALU.mult,
                        [Bxc[i], cb("normg"), Brstd], [BhT])
                if stage == 1:
                    dump("hT0", hT[:, 0, :], T, [BhT])
                    dump("hT7", hT[:, 7, :], T, [BhT])
            barrier()

            if stage >= 2 and 2 not in skip:
                mqT = [aview(4096 * h_, [64, T], BF16) for h_ in range(4)]
                szm = [aview(16384, [128, T], BF16), aview(20480, [128, T], BF16)]
                pexp = [aview(24576, [128, 4, 256], BF16), aview(26624, [128, 4, 256], BF16)]
                pT = [aview(28672, [128, 8, 128], BF16), aview(30720, [128, 8, 128], BF16)]
                om = [aview(32768, [128, 256], BF16), aview(33280, [128, 256], BF16)]
                mstat = [aview(33792, [128, 16], F32), aview(33856, [128, 16], F32)]
                memTc = aview(40960, [128, NKC * 256], F32)
                msq = [aview(49152, [128, 256], BF16), aview(50176, [128, 256], BF16)]
                mrstd = aview(51200, [128, 256], F32)
                memnT = aview(52224, [128, NKC, 256], BF16)
                wmkvf = aview(56320, [128, NKC * 512], F32)
                wmkvb = aview(72704, [128, NKC, 512], BF16)
                mkT = [aview(80896 + 512 * h_, [64, 256], BF16) for h_ in range(4)]
                mv = [aview(82944, [128, 256], BF16), aview(83456, [128, 256], BF16)]
                BmqT = [Buf("mqT%d" % h_) for h_ in range(4)]
                Bszm = [Buf("szm0"), Buf("szm1")]
                Bpexp = [Buf("pexp0"), Buf("pexp1")]
                BpT = [Buf("pT0"), Buf("pT1")]
                Bom = [Buf("om0"), Buf("om1")]
                Bmstat = [Buf("mstat0"), Buf("mstat1")]
                BmemTc, Bmrstd, BmemnT, Bwmkvf, Bwmkvb = Buf("memTc"), Buf("mrstd"), Buf("memnT"), Buf("wmkvf"), Buf("wmkvb")
                Bmsq = [Buf("msq0"), Buf("msq1")]
                BmkT = [Buf("mkT%d" % h_) for h_ in range(4)]
                Bmv = [Buf("mv0"), Buf("mv1")]

                P.dma('sp', memTc[:], d_memT, writes=[BmemTc])
                P.dma('sp', wmkvf[:], d_wmkv, writes=[Bwmkvf])
                for kc in range(NKC):
                    i = kc % 2
                    act(msq[i][:], memTc[:, kc * 256:(kc + 1) * 256], AF.Square, [BmemTc], [Bmsq[i]])
                    mm(ps[0][:, 0:256], onesb[:], msq[i][:], kc == 0, kc == NKC - 1, [cb("onesb"), Bmsq[i]], [psB[0]], sig=True)
                act(mrstd[:], ps[0][:, 0:256], AF.Ln, [psB[0]], [Bmrstd], scale=1.0 / D, bias=EPS)
                act(mrstd[:], mrstd[:], AF.Exp, [Bmrstd], [Bmrstd], scale=-0.5)
                for kc in range(NKC):
                    stt(memnT[:, kc, :], memTc[:, kc * 256:(kc + 1) * 256], memg[:, kc:kc + 1], mrstd[:],
                        ALU.mult, ALU.mult, [BmemTc, cb("memg"), Bmrstd], [BmemnT])
                act(wmkvb[:].rearrange("p a b -> p (a b)"), wmkvf[:], AF.Copy, [Bwmkvf], [Bwmkvb])
                if sub == 1:
                    dump('memnT', memnT[:, 0, :], 256, [BmemnT])
                    raise StopBuild()
                for h_ in range(4):
                    for kc in range(NKC):
                        mm(ps[1][0:64, 0:256], wmkvb[:, kc, h_ * 64:(h_ + 1) * 64], memnT[:, kc, :], kc == 0, kc == NKC - 1,
                           [Bwmkvb, BmemnT], [psB[1]])
                    cp('dve', mkT[h_][:], ps[1][0:64, 0:256], [psB[1]], [BmkT[h_]])
                for mb in range(2):
                    for kc in range(NKC):
                        mm(ps[2][:, 0:256], memnT[:, kc, mb * 128:(mb + 1) * 128], wmkvb[:, kc, 256:512], kc == 0, kc == NKC - 1,
                           [Bwmkvb, BmemnT], [psB[2]])
                    cp('dve', mv[mb][:], ps[2][:, 0:256], [psB[2]], [Bmv[mb]])
                if sub == 2:
                    dump('mkT0', mkT[3][:], 256, [BmkT[3]])
                    dump('mv1', mv[1][:], 256, [Bmv[1]])
                    raise StopBuild()
                pctr = 0
                for hp in range(2):
                    w, Bw = load_w(d_wmain[BLK_MQ + hp])
                    for hh in range(2):
                        h_ = hp * 2 + hh
                        for tg in range(NTG):
                            pi = pctr % 2
                            pctr += 1
                            for kc in range(NKC):
                                mm(ps[pi][0:64, 0:512], w[:, kc, hh * 64:(hh + 1) * 64], hT[:, kc, tg * 512:(tg + 1) * 512],
                                   kc == 0, kc == NKC - 1, [Bw, BhT], [psB[pi]])
                            act(mqT[h_][:, tg * 512:(tg + 1) * 512], ps[pi][0:64, 0:512], AF.Copy, [psB[pi]], [BmqT[h_]])
                    w, Bw = load_w(d_wmain[BLK_MZ + hp])
                    for tg in range(NTG):
                        pi = pctr % 2
                        pctr += 1
                        proj_fm(w, Bw, tg, ps[pi], psB[pi])
                        silu2(szm[hp][:, tg * 512:(tg + 1) * 512], ps[pi][:, 0:512], 512, [psB[pi]], [Bszm[hp]])
                if sub == 3:
                    dump('mqT1', mqT[3][:], 2048, [BmqT[3]])
                    dump('szm0', szm[0][:], 2048, [Bszm[0]])
                    raise StopBuild()
                def memtile(j):
                    b = j % 2
                    tsl = slice(j * 128, (j + 1) * 128)
                    pS = [ps[2 + 2 * b], ps[3 + 2 * b]]
                    BpS = [psB[2 + 2 * b], psB[3 + 2 * b]]
                    for h in range(4):
                        mm(pS[h // 2][:, (h % 2) * 256:(h % 2 + 1) * 256], mqT[h][:, tsl], mkT[h][:, :],
                           True, True, [BmqT[h], BmkT[h]], [BpS[h // 2]])
                    yield
                    mx, nmx, ssum, rs = (mstat[b][:, 0:4], mstat[b][:, 4:8], mstat[b][:, 8:12], mstat[b][:, 12:16])
                    if sub == 41:
                        dump('pS0', pS[0][:, 0:512], 512, [BpS[0]])
                        dump('pS1', pS[1][:, 0:512], 512, [BpS[1]])
                        raise StopBuild()
                    for q in range(2):
                        P.op('dve', (lambda o_, i_: (lambda e: e.tensor_reduce(o_, i_, AX.X, ALU.max)))(
                            mx[:, 2 * q:2 * q + 2], pS[q][:, 0:512].rearrange("p (a b) -> p a b", a=2)),
                            [BpS[q]], [Bmstat[b]])
                    ts('dve', nmx, mx, -0.125, None, ALU.mult, None, [Bmstat[b]], [Bmstat[b]])
                    if sub == 42:
                        dump('mstat', mstat[b][:], 16, [Bmstat[b]])
                        raise StopBuild()
                    yield
                    for h in range(4):
                        act(pexp[b][:, h, :], pS[h // 2][:, (h % 2) * 256:(h % 2 + 1) * 256], AF.Exp,
                            [BpS[h // 2], Bmstat[b]], [Bpexp[b], Bmstat[b]], scale=0.125, bias=nmx[:, h:h + 1],
                            accum=ssum[:, h:h + 1])
                    P.op('dve', (lambda o_, i_: (lambda e: e.reciprocal(o_, i_)))(rs, ssum), [Bmstat[b]], [Bmstat[b]])
                    if sub == 4:
                        dump('mstat', mstat[b][:], 16, [Bmstat[b]])
                        dump('pexp', pexp[b][:].rearrange('p a b -> p (a b)'), 1024, [Bpexp[b]])
                        raise StopBuild()
                    yield
                    pTp = ps[6 + b][:].bitcast(BF16)
                    for h in range(4):
                        for mb in range(2):
                            c0 = (h * 2 + mb) * 128
                            tr(pTp[:, c0:c0 + 128], pexp[b][:, h, mb * 128:(mb + 1) * 128], identb[:],
                               [Bpexp[b], cb("identb")], [psB[6 + b]], sig=(h == 3 and mb == 1))
                    act(pT[b][:].rearrange("p a b -> p (a b)"), pTp[:, 0:1024], AF.Copy, [psB[6 + b]], [BpT[b]])
                    yield
                    pO = ps[b]
                    for h in range(4):
                        for mb in range(2):
                            mm(pO[:, h * 64:(h + 1) * 64], pT[b][:, h * 2 + mb, :], mv[mb][:, h * 64:(h + 1) * 64],
                               mb == 0, mb == 1, [BpT[b], Bmv[mb]], [psB[b]], sig=(h == 3 and mb == 1))
                    tt('dve', om[b][:].rearrange("p (a b) -> p a b", a=4), pO[:, 0:256].rearrange("p (a b) -> p a b", a=4),
                       rs.unsqueeze(2).to_broadcast([128, 4, 64]), ALU.mult, [psB[b], Bmstat[b]], [Bom[b]])
                    if sub == 5:
                        dump('om', om[b][:], 256, [Bom[b]])
                        raise StopBuild()
                    yield
                    pT2 = ps[6 + b][:].bitcast(BF16)
                    for hp in range(2):
                        tr(pT2[:, hp * 128:(hp + 1) * 128], om[b][:, hp * 128:(hp + 1) * 128], identb[:],
                           [Bom[b], cb("identb")], [psB[6 + b]], sig=(hp == 1))
                    for hp in range(2):
                        stt(oT_m[:, hp, tsl], pT2[:, hp * 128:(hp + 1) * 128], 0.5, szm[hp][:, tsl], ALU.mult, ALU.mult,
                            [psB[6 + b], Bszm[hp]], [BoT_m])
                    yield

                for j0 in range(0, NT if sub == 0 else 2, 2):
                    gl = [memtile(j0), memtile(j0 + 1)]
                    while gl:
                        for g_ in list(gl):
                            try:
                                next(g_)
                            except StopIteration:
                                gl.remove(g_)
                if stage == 2:
                    dump("oTm0", oT_m[:, 0, :], T, [BoT_m])
                    dump("oTm1", oT_m[:, 1, :], T, [BoT_m])
                barrier()

            if stage >= 3 and 3 not in skip:
                NRS = 3
                qT = [aview(0, [128, T], BF16), aview(4096, [128, T], BF16)]
                kT = [aview(8192, [128, T], BF16), aview(12288, [128, T], BF16)]
                vTM = [aview(16384, [128, NT, 128], BF16), aview(20480, [128, NT, 128], BF16)]
                sz2 = [aview(24576, [128, T], BF16), aview(28672, [128, T], BF16)]
                oacc = [aview(32768, [128, T], BF16), aview(36864, [128, T], BF16)]
                RB = 40960
                RSZ = 8208 + 8208 + 4096 + 4096 + 16 + 512
                spb = [aview(RB + s_ * RSZ, [128, T + 4], F32) for s_ in range(NRS)]
                Cpad = [aview(RB + s_ * RSZ + 8208, [128, T + 4], F32) for s_ in range(NRS)]
                att = [aview(RB + s_ * RSZ + 16416, [128, T], BF16) for s_ in range(NRS)]
                attT = [aview(RB + s_ * RSZ + 20512, [128, NT, 128], BF16) for s_ in range(NRS)]
                sbst = [aview(RB + s_ * RSZ + 24608, [128, 4], F32) for s_ in range(NRS)]
                junk = [aview(RB + s_ * RSZ + 24624, [128, 128], F32) for s_ in range(NRS)]
                assert RB + NRS * RSZ <= AR_BYTES
                BqT = [Buf("qT0"), Buf("qT1")]
                BkT = [Buf("kT0"), Buf("kT1")]
                BvTM = [Buf("vTM0"), Buf("vTM1")]
                Bsz2 = [Buf("sz20"), Buf("sz21")]
                Boacc = [Buf("oacc0"), Buf("oacc1")]
                Bspb = [Buf("spb%d" % s_) for s_ in range(NRS)]
                BCpad = [Buf("Cpad%d" % s_) for s_ in range(NRS)]
                Batt = [Buf("att%d" % s_) for s_ in range(NRS)]
                BattT = [Buf("attT%d" % s_) for s_ in range(NRS)]
                Bsbst = [Buf("sbst%d" % s_) for s_ in range(NRS)]
                Bjunk = [Buf("junk%d" % s_) for s_ in range(NRS)]
                for s_ in range(NRS):
                    memset('pool', Cpad[s_][:, 0:1], 0.0, [BCpad[s_]])
                    memset('pool', spb[s_][:, 0:1], 0.0, [Bspb[s_]])
                sbc = {'p': 0, 't': 0}

                def sb_proj(h):
                    hb = h % 2
                    wq, Bwq = load_w(d_wmain[BLK_SBQ + h])
                    for tg in range(NTG):
                        pi = sbc['p'] % 2
                        sbc['p'] += 1
                        proj_fm(wq, Bwq, tg, ps[pi], psB[pi])
                        act(qT[hb][:, tg * 512:(tg + 1) * 512], ps[pi][:, 0:512], AF.Copy, [psB[pi]], [BqT[hb]],
                            scale=float(128 ** -0.5))
                        yield
                    wk, Bwk = load_w(d_wmain[BLK_SBK + h])
                    for tg in range(NTG):
                        pi = sbc['p'] % 2
                        sbc['p'] += 1
                        proj_fm(wk, Bwk, tg, ps[pi], psB[pi])
                        cp('dve', kT[hb][:, tg * 512:(tg + 1) * 512], ps[pi][:, 0:512], [psB[pi]], [BkT[hb]])
                        yield
                    wz, Bwz = load_w(d_wmain[BLK_SBZ + h])
                    for tg in range(NTG):
                        pi = sbc['p'] % 2
                        sbc['p'] += 1
                        proj_fm(wz, Bwz, tg, ps[pi], psB[pi])
                        silu2(sz2[hb][:, tg * 512:(tg + 1) * 512], ps[pi][:, 0:512], 512, [psB[pi]], [Bsz2[hb]])
                        yield
                    wv, Bwv = load_w(d_wmain[BLK_SBV + h])
                    for j4 in range(4):
                        pi = sbc['p'] % 2
                        sbc['p'] += 1
                        for jj in range(4):
                            j = j4 * 4 + jj
                            for kc in range(NKC):
                                mm(ps[pi][:, jj * 128:(jj + 1) * 128], hT[:, kc, j * 128:(j + 1) * 128], wv[:, kc, :],
                                   kc == 0, kc == NKC - 1, [BhT, Bwv], [psB[pi]], sig=(kc == NKC - 1 and jj == 3))
                        cp('dve', vTM[hb][:, j4 * 4:(j4 + 1) * 4, :].rearrange("p a b -> p (a b)"), ps[pi][:, 0:512],
                           [psB[pi]], [BvTM[hb]])
                        yield

                def sb_row(h, i, rb):
                    hb = h % 2
                    nk = i + 1
                    L = nk * 128
                    qsl = slice(i * 128, (i + 1) * 128)
                    zb = 2 + rb
                    for c in range((L + 511) // 512):
                        cols = min(512, L - c * 512)
                        c0 = c * 512
                        mm(ps[zb][:, 0:cols], qT[hb][:, qsl], kT[hb][:, c0:c0 + cols], True, True,
                           [BqT[hb], BkT[hb]], [psB[zb]])
                        yield
                        act(spb[rb][:, 1 + c0:1 + c0 + cols], ps[zb][:, 0:cols], AF.Exp, [psB[zb]], [Bspb[rb]], scale=-1.0)
                        act(spb[rb][:, 1 + c0:1 + c0 + cols], spb[rb][:, 1 + c0:1 + c0 + cols], AF.Ln, [Bspb[rb]], [Bspb[rb]], bias=1.0)
                        yield
                        P.op('dve', (lambda o_, d0, d1, ini: (lambda e: e.tensor_tensor_scan(o_, d0, d1, ini, ALU.add, ALU.add)))(
                            Cpad[rb][:, 1 + c0:1 + c0 + cols], spb[rb][:, c0:c0 + cols], ps[zb][:, 0:cols],
                            Cpad[rb][:, c0:c0 + 1]), [Bspb[rb], psB[zb], BCpad[rb]], [BCpad[rb]])
                        yield
                    ctot, nct = sbst[rb][:, 0:1], sbst[rb][:, 1:2]
                    tt('dve', junk[rb][:], Cpad[rb][:, i * 128:i * 128 + 128], spb[rb][:, i * 128:i * 128 + 128], ALU.add,
                       [BCpad[rb], Bspb[rb]], [Bjunk[rb]])
                    P.op('dve', (lambda o_, a_, b_, acc: (lambda e: e.scalar_tensor_tensor(
                        o_, a_, 1.0, b_, ALU.mult, ALU.mult, accum_out=acc)))(
                        junk[rb][:], junk[rb][:], identf[:], ctot),
                        [Bjunk[rb], cb("identf")], [Bjunk[rb], Bsbst[rb]])
                    ts('dve', nct, ctot, -1.0, None, ALU.mult, None, [Bsbst[rb]], [Bsbst[rb]])
                    ts('dve', Cpad[rb][:, 1 + i * 128:1 + L], Cpad[rb][:, 1 + i * 128:1 + L], ctot, None, ALU.min, None,
                       [BCpad[rb], Bsbst[rb]], [BCpad[rb]])
                    yield
                    act(att[rb][:, 0:L], Cpad[rb][:, 1:1 + L], AF.Exp, [BCpad[rb], Bsbst[rb]], [Batt[rb]], bias=nct)
                    yield
                    tt('pool', att[rb][:, i * 128:L], att[rb][:, i * 128:L], causb[:], ALU.mult,
                       [Batt[rb], cb("causb")], [Batt[rb]])
                    yield
                    for g8 in range((nk + 7) // 8):
                        tb = 5 + (sbc['t'] % 2)
                        sbc['t'] += 1
                        n8 = min(8, nk - g8 * 8)
                        ptb = ps[tb][:].bitcast(BF16)
                        for m8 in range(n8):
                            m = g8 * 8 + m8
                            tr(ptb[:, m8 * 128:(m8 + 1) * 128], att[rb][:, m * 128:(m + 1) * 128], identb[:],
                               [Batt[rb], cb("identb")], [psB[tb]], sig=(m8 == n8 - 1))
                        cp('dve', attT[rb][:, g8 * 8:g8 * 8 + n8, :].rearrange("p a b -> p (a b)"), ptb[:, 0:n8 * 128],
                           [psB[tb]], [BattT[rb]])
                        yield
                    for m in range(nk):
                        mm(ps[7][:, 0:128], vTM[hb][:, m, :], attT[rb][:, m, :], m == 0, m == nk - 1,
                           [BvTM[hb], BattT[rb]], [psB[7]])
                    stt(oacc[hb][:, qsl], ps[7][:, 0:128], 0.5, sz2[hb][:, qsl], ALU.mult, ALU.mult,
                        [psB[7], Bsz2[hb]], [Boacc[hb]])
                    yield

                def step(g_):
                    try:
                        next(g_)
                        return True
                    except StopIteration:
                        return False

                nheads = 8 if stage > 3 else 2
                pg = sb_proj(0)
                while step(pg):
                    pass
                for h in range(nheads):
                    pg = sb_proj(h + 1) if h + 1 < nheads else None
                    order = [15, 0, 14, 1, 13, 2, 12, 3, 11, 4, 10, 5, 9, 6, 8, 7]
                    active = []
                    free_sets = list(range(NRS))
                    nxt = 0
                    while nxt < NT or active:
                        while free_sets and nxt < NT:
                            s_ = free_sets.pop(0)
                            active.append((sb_row(h, order[nxt], s_), s_))
                            nxt += 1
                        for (g_, s_) in list(active):
                            if not step(g_):
                                active.remove((g_, s_))
                                free_sets.append(s_)
                        if pg is not None and not step(pg):
                            pg = None
                    while pg is not None and step(pg):
                        pass
                    P.dma('sp', d_oTsb[h], oacc[h % 2][:], reads=[Boacc[h % 2]], writes=[BoT_sb[h]])
                if stage == 3:
                    dump("oacc0", oacc[0][:], T, [Boacc[0]])
                    dump("oacc1", oacc[1][:], T, [Boacc[1]])
                barrier()

            if stage >= 4:
                BB = {}

                def B(name):
                    if name not in BB:
                        BB[name] = Buf(name)
                    return BB[name]

                QS = float(128 ** -0.5)
                TB2X = 24992 + 8 * 8192 + 17424 + 6144 + 3136 + 2048
                assert TB2X + 512 <= AR_BYTES
                halo = aview(0, [128, 72], F32)
                S = aview(320, [128, 8, 128], F32)
                Sbf = aview(4416, [128, 8, 128], BF16)
                vnE = aview(6464, [128, 8, 128], BF16)
                vnO = aview(8512, [128, 8, 128], BF16)
                t_b, t_a, t_G, t_eG, t_kd = [aview(10560 + 2048 * i_, [8, 512], F32) for i_ in range(5)]
                t_eGb = aview(10560 + 2048 * 3, [8, 512], BF16)
                t_kdb = aview(10560 + 2048 * 3 + 1024, [8, 512], BF16)
                tab = aview(20800, [128, 64], F32)
                ntabs = [aview(21056, [128, 64], F32), aview(TB2X, [128, 64], F32)]
                dec_bcs = [aview(21312, [128, 64], F32), aview(TB2X + 256, [128, 64], F32)]
                wbab = aview(21568, [128, 8, 16], BF16)
                nA = aview(21824, [8, 1], F32)
                decF = aview(21856, [8, 8], F32)
                ident4 = aview(21888, [128, 512], BF16)
                offd4 = aview(22912, [128, 512], BF16)
                maskb4 = aview(23936, [128, 512], BF16)
                mh1 = aview(24960, [128, 1], F32)
                dnngh = aview(24964, [128, 1], F32)
                OPB = 24992

                def slotv(slot):
                    b_ = OPB + slot * 8192
                    return dict(wpT=aview(b_, [128, 512], BF16), qdT=aview(b_ + 1024, [128, 512], BF16),
                                aT=aview(b_ + 2048, [128, 512], BF16), kdTM=aview(b_ + 3072, [128, 4, 128], BF16),
                                ub=aview(b_ + 4096, [128, 4, 128], F32), sz2=aview(b_ + 6144, [128, 512], BF16),
                                oacc=aview(b_ + 7168, [128, 512], BF16))
                TB = OPB + 8 * 8192
                xpre = aview(TB, [128, 516], F32)
                sqf = xpre
                accb = aview(TB + 2064, [128, 512], F32)
                rq = aview(TB + 4112, [128, 512], F32)
                kf = aview(TB + 6160, [128, 512], F32)
                kTn = aview(TB + 8208, [128, 512], BF16)
                qnT = aview(TB + 9232, [128, 512], BF16)
                vT2 = aview(TB + 10256, [128, 512], BF16)
                keT = aview(TB + 11280, [128, 512], BF16)
                kdT = aview(TB + 12304, [128, 512], BF16)
                vTM = aview(TB + 13328, [128, 4, 128], BF16)
                keTM = aview(TB + 14352, [128, 4, 128], BF16)
                Ebuf = aview(TB + 15376, [128, 512], F32)
                chb = [aview(TB + 17424 + 1024 * i_, [128, 512], BF16) for i_ in range(6)]
                TB2 = TB + 17424 + 6144
                o_tile = [aview(TB2 + 512 * i_, [128, 128], F32) for i_ in range(4)]
                onb = [aview(TB2 + 2048 + 256 * i_, [128, 128], BF16) for i_ in range(4)]
                ost = [aview(TB2 + 3072 + 16 * i_, [128, 4], F32) for i_ in range(4)]
                mhalf4 = aview(TB2 + 3136, [128, 512], F32)
                assert TB2 + 3136 + 2048 <= AR_BYTES
                XT0 = TB2X + 512
                Ebuf_s = [Ebuf, aview(XT0, [128, 512], F32)]
                vTM_s = [vTM, aview(XT0 + 2048, [128, 4, 128], BF16)]
                keTM_s = [keTM, aview(XT0 + 3072, [128, 4, 128], BF16)]
                assert XT0 + 4096 <= AR_BYTES
                psR6 = [Buf("ps6_%d" % i_) for i_ in range(4)]
                psR7 = [Buf("ps7_%d" % i_) for i_ in range(4)]
                psR2 = [Buf("ps2_%d" % i_) for i_ in range(4)]

                memset('pool', halo[:], 0.0, [B("halo")])
                memset('pool', S[:].rearrange("p a b -> p (a b)"), 0.0, [B("S%d" % h_) for h_ in range(8)])
                memset('pool', Sbf[:].rearrange("p a b -> p (a b)"), 0.0, [B("Sbf%d" % h_) for h_ in range(8)])
                memset('pool', vnE[:].rearrange("p a b -> p (a b)"), 0.0, [B("vnE%d" % h_) for h_ in range(8)])
                memset('pool', vnO[:].rearrange("p a b -> p (a b)"), 0.0, [B("vnO%d" % h_) for h_ in range(8)])
                memset('pool', mh1[:], -0.5, [B("mh1")])
                memset('pool', mhalf4[:], -0.5, [B("mh1")])
                for q_ in range(4):
                    cp('pool', ident4[:, q_ * 128:(q_ + 1) * 128], identb[:], [cb("identb")], [B("ident4")])
                    cp('pool', offd4[:, q_ * 128:(q_ + 1) * 128], offdb[:], [cb("offdb")], [B("offd4")])
                    cp('pool', maskb4[:, q_ * 128:(q_ + 1) * 128], maskbb[:], [cb("maskbb")], [B("maskb4")])
                ts('pool', dnngh[:], dnng[:], 0.5, None, ALU.mult, None, [cb("dnng")], [B("dnngh")])
                act(nA[:], alog[:], AF.Exp, [cb("alog")], [B("nA")])
                ts('pool', nA[:], nA[:], -1.0, None, ALU.mult, None, [B("nA")], [B("nA")])
                P.dma('pool', wbab[:].rearrange("p a b -> p (a b)"), d_wba, writes=[B("wbab")])
                pj = [0]

                def pbank():
                    pj[0] += 1
                    return pj[0] % 2

                def tables(tg):
                    tsl = slice(tg * 512, (tg + 1) * 512)
                    ntab, dec_bc = ntabs[tg % 2], dec_bcs[tg % 2]
                    Bnt, Bdc = B("ntab%d" % (tg % 2)), B("dec_bc%d" % (tg % 2))
                    for q_ in range(2):
                        for kc in range(NKC):
                            mm(ps[q_][0:8, 0:512], wbab[:, kc, q_ * 8:(q_ + 1) * 8], hT[:, kc, tsl], kc == 0, kc == NKC - 1,
                               [B("wbab"), BhT], [psB[q_]])
                    act(t_b[:], ps[0][0:8, 0:512], AF.Exp, [psB[0]], [B("t_b")], scale=-1.0)
                    ts('dve', t_b[:], t_b[:], 1.0, None, ALU.add, None, [B("t_b")], [B("t_b")])
                    P.op('dve', lambda e: e.reciprocal(t_b[:], t_b[:]), [B("t_b")], [B("t_b")])
                    act(t_a[:], ps[1][0:8, 0:512], AF.Exp, [psB[1], cb("dtb")], [B("t_a")], bias=dtb[:, 0:1])
                    act(t_a[:], t_a[:], AF.Ln, [B("t_a")], [B("t_a")], bias=1.0)
                    ts('dve', t_a[:], t_a[:], nA[:, 0:1], None, ALU.mult, None, [B("t_a"), B("nA")], [B("t_a")])
                    P.op('dve', lambda e: e.tensor_tensor_scan(t_G[:], rmask[:], t_a[:], 0.0, ALU.mult, ALU.add),
                         [B("t_a"), cb("rmask")], [B("t_G")])
                    act(t_eGb[:], t_G[:], AF.Exp, [B("t_G")], [B("t_eG")])
                    G3 = t_G[:].rearrange("p (n c) -> p n c", c=64)
                    tt('dve', t_kd[:].rearrange("p (n c) -> p n c", c=64), G3[:, :, 63:64].to_broadcast([8, 8, 64]), G3,
                       ALU.subtract, [B("t_G")], [B("t_kd")])
                    act(t_kdb[:], t_kd[:], AF.Exp, [B("t_kd")], [B("t_kdb")])
                    act(decF[:].rearrange("p (n c) -> p n c", c=1), G3[:, :, 63:64], AF.Exp, [B("t_G")], [B("decF")])
                    for tile in range(4):
                        for q_, src, bn in ((0, t_b, "t_b"), (1, t_G, "t_G")):
                            c0 = (tile * 2 + q_) * 8
                            mm(ps[2][:, c0:c0 + 8], src[:, tile * 128:(tile + 1) * 128], identf[0:8, 0:8], True, True,
                               [B(bn), cb("identf")], [psB[2]], sig=(tile == 3 and q_ == 1))
                    cp('dve', tab[:], ps[2][:, 0:64], [psB[2]], [B("tab")])
                    ts('pool', ntab[:], tab[:], -1.0, None, ALU.mult, None, [B("tab")], [Bnt])
                    for h_ in range(8):
                        mm(ps[2][:, 64 + h_ * 8:72 + h_ * 8], esel[:, h_ * 128:(h_ + 1) * 128], decF[:], True, True,
                           [cb("esel"), B("decF")], [psB[2]], sig=(h_ == 7))
                    cp('dve', dec_bc[:], ps[2][:, 64:128], [psB[2]], [Bdc])

                def silu2e(out, y, reads, writes):
                    i = sctr[0] % 2
                    sctr[0] += 1
                    tmp = silu_tmp[i][:, 0:512]
                    act(tmp, y, AF.Exp, reads, [Bsilu_tmp[i]], scale=-1.0)
                    act(tmp, tmp, AF.Ln, [Bsilu_tmp[i]], [Bsilu_tmp[i]], bias=1.0)
                    act(tmp, tmp, AF.Exp, [Bsilu_tmp[i]], [Bsilu_tmp[i]], scale=-1.0)
                    stt(out, y, 2.0, tmp, ALU.mult, ALU.mult, list(reads) + [Bsilu_tmp[i]], writes)

                def bcast(h, src, bn):
                    pi = pbank()
                    mm(ps[pi][:, 0:512], eselb[:, h * 128:(h + 1) * 128], src[:], True, True, [cb("eselb"), B(bn)], [psB[pi]])
                    return pi

                def proj(h, tg, slot):
                    ops = slotv(slot)
                    sn = "s%d_" % slot
                    for qi, nm in enumerate("qkv"):
                        cblk = qi * 8 + h
                        w, Bw = load_w(d_wmain[cblk])
                        pi = pbank()
                        proj_fm(w, Bw, tg, ps[pi], psB[pi])
                        act(xpre[:, 0:3], halo[:, cblk * 3:cblk * 3 + 3], AF.Copy, [B("halo")], [B("xpre")])
                        act(xpre[:, 3:515], ps[pi][:, 0:512], AF.Copy, [psB[pi]], [B("xpre")])
                        act(halo[:, cblk * 3:cblk * 3 + 3], xpre[:, 512:515], AF.Copy, [B("xpre")], [B("halo")])
                        yield
                        ts('dve', accb[:], xpre[:, 3:515], convw[:, cblk * 4 + 3:cblk * 4 + 4], None, ALU.mult, None,
                           [B("xpre"), cb("convw")], [B("acc")])
                        for j_ in range(3):
                            stt(accb[:], xpre[:, j_:j_ + 512], convw[:, cblk * 4 + j_:cblk * 4 + j_ + 1], accb[:], ALU.mult, ALU.add,
                                [B("xpre"), cb("convw"), B("acc")], [B("acc")])
                        yield
                        silu2e(accb[:], accb[:], [B("acc")], [B("acc")])
                        yield
                        if nm in "qk":
                            sqb = aview(TB, [128, 512], BF16)
                            act(sqb[:], accb[:], AF.Square, [B("acc")], [B("xpre")])
                            pi = pbank()
                            mm(ps[pi][:, 0:512], onesb[:], sqb[:], True, True, [cb("onesb"), B("xpre")], [psB[pi]])
                            act(rq[:], ps[pi][:, 0:512], AF.Ln, [psB[pi]], [B("rq")], bias=4.0 * EPS)
                            act(rq[:], rq[:], AF.Exp, [B("rq")], [B("rq")], scale=-0.5)
                            yield
                            if nm == "k":
                                tt('dve', kf[:], accb[:], rq[:], ALU.mult, [B("acc"), B("rq")], [B("kf")])
                                act(kTn[:], kf[:], AF.Copy, [B("kf")], [B("kTn")])
                                pi = bcast(h, t_eGb, "t_eG")
                                tt('dve', keT[:], kf[:], ps[pi][:, 0:512], ALU.mult, [B("kf"), psB[pi]], [B("keT")])
                                pi = bcast(h, t_kdb, "t_kdb")
                                tt('dve', kdT[:], kf[:], ps[pi][:, 0:512], ALU.mult, [B("kf"), psB[pi]], [B("kdT")])
                            else:
                                stt(kf[:], accb[:], QS, rq[:], ALU.mult, ALU.mult, [B("acc"), B("rq")], [B("kf")])
                                act(qnT[:], kf[:], AF.Copy, [B("kf")], [B("qnT")])
                                pi = bcast(h, t_eGb, "t_eG")
                                tt('dve', ops["qdT"][:], kf[:], ps[pi][:, 0:512], ALU.mult, [B("kf"), psB[pi]], [B(sn + "qdT")])
                        else:
                            act(vT2[:], accb[:], AF.Copy, [B("acc")], [B("vT2")])
                        yield
                    w, Bw = load_w(d_wmain[BLK_DNZ + h])
                    pi = pbank()
                    proj_fm(w, Bw, tg, ps[pi], psB[pi])
                    silu2e(ops["sz2"][:], ps[pi][:, 0:512], [psB[pi]], [B(sn + "sz2")])
                    yield
                    u_ = slot % 2
                    ntab_ = ntabs[tg % 2]
                    tcs_ = [slice(t_ * 128, (t_ + 1) * 128) for t_ in range(4)]
                    pi = pbank()
                    pb_ = ps[pi][:].bitcast(BF16)
                    for t_ in range(4):
                        tr(pb_[:, t_ * 128:(t_ + 1) * 128], vT2[:, tcs_[t_]], identb[:], [B("vT2"), cb("identb")], [psB[pi]], sig=False)
                    for t_ in range(4):
                        tr(pb_[:, 512 + t_ * 128:512 + (t_ + 1) * 128], keT[:, tcs_[t_]], identb[:], [B("keT"), cb("identb")], [psB[pi]],
                           sig=(t_ == 3))
                    act(vTM_s[u_][:].rearrange("p a b -> p (a b)"), pb_[:, 0:512], AF.Copy, [psB[pi]], [B("vTM%d" % u_)], scale=0.5)
                    act(keTM_s[u_][:].rearrange("p a b -> p (a b)"), pb_[:, 512:1024], AF.Copy, [psB[pi]], [B("keTM%d" % u_)])
                    yield
                    pi = pbank()
                    pb_ = ps[pi][:].bitcast(BF16)
                    for t_ in range(4):
                        tr(pb_[:, t_ * 128:(t_ + 1) * 128], kdT[:, tcs_[t_]], identb[:], [B("kdT"), cb("identb")], [psB[pi]], sig=(t_ == 3))
                    cp('dve', ops["kdTM"][:].rearrange("p a b -> p (a b)"), pb_[:, 0:512], [psB[pi]], [B(sn + "kdTM")])
                    yield

                def egen(h, tg, slot):
                    u_ = slot % 2
                    ntab_ = ntabs[tg % 2]
                    tcs_ = [slice(t_ * 128, (t_ + 1) * 128) for t_ in range(4)]
                    yield
                    pi = pbank()
                    mm(ps[pi][:, 0:512], esel[:, h * 128:(h + 1) * 128], t_G[:], True, False, [cb("esel"), B("t_G")], [psB[pi]], sig=False)
                    mm(ps[pi][:, 0:512], identb[:], maskb4[:], False, True, [cb("identb"), B("maskb4")], [psB[pi]])
                    for t_ in range(4):
                        act(Ebuf_s[u_][:, tcs_[t_]], ps[pi][:, tcs_[t_]], AF.Exp, [psB[pi], B("ntab%d" % (tg % 2))], [B("E%d" % u_)],
                            bias=ntab_[:, t_ * 16 + 8 + h:t_ * 16 + 9 + h])
                    yield

                def chain(h, tg, slot):
                    ops = slotv(slot)
                    ntab = ntabs[tg % 2]
                    Bnt = B("ntab%d" % (tg % 2))
                    sn = "s%d_" % slot
                    Nb, Mb, P0, P1, M2b, N2b = chb
                    BN, BM, BP0, BP1, BM2, BN2 = [B("ch%d" % i_) for i_ in range(6)]
                    tcs = [slice(t_ * 128, (t_ + 1) * 128) for t_ in range(4)]
                    for t_ in range(4):
                        mm(ps[3][:, tcs[t_]], kTn[:, tcs[t_]], kTn[:, tcs[t_]], True, True, [B("kTn")], [psB[3]], sig=(t_ == 3))
                    for t_ in range(4):
                        mm(ps[4][:, tcs[t_]], kTn[:, tcs[t_]], qnT[:, tcs[t_]], True, True, [B("kTn"), B("qnT")], [psB[4]], sig=(t_ == 3))
                    u_ = slot % 2
                    Ebuf, vTM, keTM = Ebuf_s[u_], vTM_s[u_], keTM_s[u_]
                    BE, BvTM_, BkeTM_ = B("E%d" % u_), B("vTM%d" % u_), B("keTM%d" % u_)
                    yield
                    for t_ in range(4):
                        stt(Nb[:, tcs[t_]], ps[3][:, tcs[t_]], tab[:, t_ * 16 + h:t_ * 16 + h + 1], Ebuf[:, tcs[t_]], ALU.mult, ALU.mult,
                            [psB[3], B("tab"), BE], [BN])
                    tt('dve', Nb[:], Nb[:], offd4[:], ALU.mult, [BN, B("offd4")], [BN])
                    yield
                    tt('dve', ops["aT"][:], ps[4][:, 0:512], Ebuf[:], ALU.mult, [psB[4], BE], [B(sn + "aT")])
                    yield
                    b5 = ps[5][:].bitcast(BF16)
                    for t_ in range(4):
                        tr(b5[:, t_ * 128:(t_ + 1) * 128], Nb[:, tcs[t_]], identb[:], [BN, cb("identb")], [psB[5]], sig=(t_ == 3))
                    yield
                    cp('dve', Mb[:], b5[:, 0:512], [psB[5]], [BM])
                    tt('dve', P0[:], ident4[:], Nb[:], ALU.subtract, [B("ident4"), BN], [BP0])
                    yield
                    cur, Bcur, oth, Both = P0, BP0, P1, BP1
                    for t_ in range(4):
                        mm(ps[3][:, tcs[t_]], Nb[:, tcs[t_]], Mb[:, tcs[t_]], True, True, [BN, BM], [psB[3]], sig=(t_ == 3))
                    for t_ in range(4):
                        mm(ps[4][:, tcs[t_]], Mb[:, tcs[t_]], Nb[:, tcs[t_]], True, True, [BN, BM], [psB[4]], sig=(t_ == 3))
                    yield
                    act(M2b[:], ps[3][:, 0:512], AF.Copy, [psB[3]], [BM2])
                    cp('dve', N2b[:], ps[4][:, 0:512], [psB[4]], [BN2])
                    Nb, N2b, BN, BN2 = N2b, Nb, BN2, BN
                    Mb, M2b, BM, BM2 = M2b, Mb, BM2, BM
                    yield
                    for lev in range(1, 6):
                        for t_ in range(4):
                            mm(ps[5][:, tcs[t_]], Mb[:, tcs[t_]], cur[:, tcs[t_]], True, True, [BM, Bcur], [psB[5]], sig=(t_ == 3))
                        if lev < 5:
                            for t_ in range(4):
                                mm(ps[3][:, tcs[t_]], Nb[:, tcs[t_]], Mb[:, tcs[t_]], True, True, [BN, BM], [psB[3]], sig=(t_ == 3))
                            if lev < 4:
                                for t_ in range(4):
                                    mm(ps[4][:, tcs[t_]], Mb[:, tcs[t_]], Nb[:, tcs[t_]], True, True, [BN, BM], [psB[4]], sig=(t_ == 3))
                        yield
                        tt('dve', oth[:], ps[5][:, 0:512], cur[:], ALU.add, [psB[5], Bcur], [Both])
                        cur, Bcur, oth, Both = oth, Both, cur, Bcur
                        if lev < 5:
                            act(M2b[:], ps[3][:, 0:512], AF.Copy, [psB[3]], [BM2])
                            if lev < 4:
                                cp('dve', N2b[:], ps[4][:, 0:512], [psB[4]], [BN2])
                            Nb, N2b, BN, BN2 = N2b, Nb, BN2, BN
                            Mb, M2b, BM, BM2 = M2b, Mb, BM2, BM
                        yield
                    for t_ in range(4):
                        mm(ps[3][:, tcs[t_]], cur[:, tcs[t_]], vTM[:, t_, :], True, True, [Bcur, BvTM_], [psB[3]], sig=(t_ == 3))
                    for t_ in range(4):
                        mm(ps[4][:, tcs[t_]], keTM[:, t_, :], cur[:, tcs[t_]], True, True, [Bcur, BkeTM_], [psB[4]], sig=(t_ == 3))
                    yield
                    for t_ in range(4):
                        ts('dve', ops["ub"][:, t_, :], ps[3][:, tcs[t_]], tab[:, t_ * 16 + h:t_ * 16 + h + 1], None, ALU.mult, None,
                           [psB[3], B("tab")], [B(sn + "ub")])
                    act(ops["wpT"][:], ps[4][:, 0:512], AF.Copy, [psB[4]], [B(sn + "wpT")])
                    yield

                def scan(h, tg, slot, hh):
                    ops = slotv(slot)
                    ntab, dec_bc = ntabs[tg % 2], dec_bcs[tg % 2]
                    Bnt, Bdc = B("ntab%d" % (tg % 2)), B("dec_bc%d" % (tg % 2))
                    sn = "s%d_" % slot
                    tsl0 = tg * 512
                    Pw = ps[6][:, hh * 128:(hh + 1) * 128]
                    Po = ps[7][:, hh * 128:(hh + 1) * 128]
                    BS, BSbf = B("S%d" % h), B("Sbf%d" % h)
                    for t_ in range(4):
                        tc = slice(t_ * 128, (t_ + 1) * 128)
                        for par in range(2):
                            ch = t_ * 2 + par
                            rows = slice(par * 64, par * 64 + 64)
                            vn = (vnE if par == 0 else vnO)[:, h, :]
                            Bvn = B(("vnE%d" if par == 0 else "vnO%d") % h)
                            mm(Pw, ops["wpT"][:, tc], Sbf[:, h, :], True, True, [B(sn + "wpT"), BSbf], [psB[6]])
                            yield
                            stt(vn[rows], Pw[rows], ntab[rows, t_ * 16 + h:t_ * 16 + h + 1], ops["ub"][rows, t_, :], ALU.mult, ALU.add,
                                [psB[6], Bnt, B(sn + "ub")], [Bvn])
                            yield
                            import os as _os
                            _ks = _os.environ.get("KSC", "PQS")
                            if "P" in _ks:
                                mm(Po, ops["qdT"][:, tc], Sbf[:, h, :], True, "Q" not in _ks, [B(sn + "qdT"), BSbf], [psB[7]], sig=("Q" not in _ks))
                            if "Q" in _ks:
                                mm(Po, ops["aT"][:, tc], vn, "P" not in _ks, True, [B(sn + "aT"), Bvn], [psB[7]])
                            if "S" in _ks:
                                mm(Pw, ops["kdTM"][:, t_, :], vn, True, True, [B(sn + "kdTM"), Bvn], [psB[6]])
                            yield
                            act(o_tile[hh][rows], Po[rows], AF.Copy, [psB[7]], [B("otile%d" % hh)])
                            stt(S[:, h, :], S[:, h, :], dec_bc[:, h * 8 + ch:h * 8 + ch + 1], Pw, ALU.mult, ALU.add,
                                [BS, Bdc, psB[6]], [BS])
                            yield
                            act(Sbf[:, h, :], S[:, h, :], AF.Copy, [BS], [BSbf])
                            yield
                        act(onb[hh][:], o_tile[hh][:], AF.Square, [B("otile%d" % hh)], [B("onb%d" % hh), B("ost%d" % hh)],
                            accum=ost[hh][:, 0:1])
                        ts('pool', ost[hh][:, 1:2], ost[hh][:, 0:1], 1.0 / 128, EPS, ALU.mult, ALU.add, [B("ost%d" % hh)], [B("ost%d" % hh)])
                        tt('pool', ost[hh][:, 2:3], ost[hh][:, 1:2], mh1[:], ALU.pow, [B("ost%d" % hh), B("mh1")], [B("ost%d" % hh)])
                        ts('dve', onb[hh][:], o_tile[hh][:], ost[hh][:, 2:3], None, ALU.mult, None,
                           [B("otile%d" % hh), B("ost%d" % hh)], [B("onb%d" % hh)])
                        p2b = ps[2][:].bitcast(BF16)[:, 256 + hh * 128:256 + (hh + 1) * 128]
                        tr(p2b, onb[hh][:], identb[:], [B("onb%d" % hh), cb("identb")], [psB[2]])
                        stt(ops["oacc"][:, tc], p2b, dnngh[:, 0:1], ops["sz2"][:, tc], ALU.mult, ALU.mult,
                            [psB[2], B("dnngh"), B(sn + "sz2")], [B(sn + "oacc")])
                        yield
                    P.dma('sp', d_oTdn[h][:, tsl0:tsl0 + 512], ops["oacc"][:], reads=[B(sn + "oacc")], writes=[BoT_dn[h]])

                def run_il(gens):
                    gens = list(gens)
                    while gens:
                        for g_ in list(gens):
                            try:
                                next(g_)
                            except StopIteration:
                                gens.remove(g_)

                if sub in (1, 2, 3) or 20 < sub < 60:
                    tables(0)
                    if sub == 1:
                        dump("tab", tab[:], 64, [B("tab")])
                        dump("dec_bc", dec_bcs[0][:], 64, [B("dec_bc0")])
                        dump("t_G", t_G[:], 512, [B("t_G")])
                        dump("t_b", t_b[:], 512, [B("t_b")])
                        raise StopBuild()
                    if 20 < sub < 30:
                        g_ = proj(0, 0, 0)
                        for _ in range(sub - 20):
                            next(g_)
                        dump("xpre", xpre[:, 0:512], 512, [B("xpre")])
                        dump("acc", accb[:], 512, [B("acc")])
                        dump("rq", rq[:], 512, [B("rq")])
                        raise StopBuild()
                    run_il([proj(0, 0, 0)])
                    if sub == 2:
                        dump("kTn", kTn[:], 512, [B("kTn")])
                        dump("qnT", qnT[:], 512, [B("qnT")])
                        dump("qdT", slotv(0)["qdT"][:], 512, [B("s0_qdT")])
                        dump("vT2", vT2[:], 512, [B("vT2")])
                        raise StopBuild()
                    if 30 < sub < 60:
                        g_ = chain(0, 0, 0)
                        for _ in range(sub - 30):
                            next(g_)
                        dump("E", Ebuf[:], 512, [B("E")])
                        dump("Nb", chb[0][:], 512, [B("ch0")])
                        dump("Mb", chb[1][:], 512, [B("ch1")])
                        dump("P0", chb[2][:], 512, [B("ch2")])
                        raise StopBuild()
                    run_il([chain(0, 0, 0)])
                    dump("ub", slotv(0)["ub"][:].rearrange("p a b -> p (a b)"), 512, [B("s0_ub")])
                    dump("wpT", slotv(0)["wpT"][:], 512, [B("s0_wpT")])
                    dump("aT", slotv(0)["aT"][:], 512, [B("s0_aT")])
                    raise StopBuild()
                if 100 <= sub < 400:
                    tables(0)
                    for h_ in range(4):
                        run_il([proj(h_, 0, h_)])
                        run_il([chain(h_, 0, h_)])
                    nh = 1 if sub < 200 else (4 if sub < 300 else 2)
                    gs = [scan(h_, 0, h_, h_) for h_ in range(nh)]
                    for _ in range(sub % 100):
                        for g_ in gs:
                            next(g_)
                    dump("S0", S[:, 0, :], 128, [B("S0")])
                    dump("vnE0", vnE[:, 0, :], 128, [B("vnE0")])
                    dump("vnO0", vnO[:, 0, :], 128, [B("vnO0")])
                    dump("otile0", o_tile[0][:], 128, [B("otile0")])
                    raise StopBuild()
                units = [(tg, h) for tg in range(NTG if stage > 4 else 1) for h in range(8)]
                nu = len(units)
                pending_scans = []
                scan_active = []
                scan_group = [None]

                def stepg(lst):
                    for g_ in list(lst):
                        try:
                            next(g_)
                        except StopIteration:
                            lst.remove(g_)

                for i_ in range(nu + 1):
                    gens = []
                    if i_ >= 1:
                        tg_, h_ = units[i_ - 1]
                        slot_ = (((i_ - 1) // 4) % 2) * 4 + (h_ % 4)
                        cg = chain(h_, tg_, slot_)
                        if i_ < nu and units[i_][1] == 0:
                            lst = [cg]
                            while lst:
                                stepg(lst)
                                stepg(scan_active)
                        else:
                            gens.append(cg)
                    if scan_active and 4 * (scan_group[0] + 2) <= i_:
                        while scan_active:
                            stepg(scan_active)
                    if i_ < nu:
                        tg, h = units[i_]
                        if h == 0:
                            tables(tg)
                        slot = ((i_ // 4) % 2) * 4 + (h % 4)
                        gens.append(proj(h, tg, slot))
                        gens.append(egen(h, tg, slot))
                    if i_ >= 5 and (i_ - 5) % 4 == 0:
                        while scan_active:
                            stepg(scan_active)
                        g0 = (i_ - 5) // 4
                        scan_group[0] = g0
                        for hh in range(4):
                            tg_, h_ = units[g0 * 4 + hh]
                            scan_active.append(scan(h_, tg_, (g0 % 2) * 4 + hh, hh))
                    while gens:
                        stepg(gens)
                        stepg(scan_active)
                while scan_active:
                    stepg(scan_active)
                ngroups = nu // 4
                done = max(0, (nu + 1 - 5 + 3) // 4) if nu + 1 > 5 else 0
                done = len([i_ for i_ in range(nu + 1) if i_ >= 5 and (i_ - 5) % 4 == 0])
                for g0 in range(done, ngroups):
                    run_il([scan(units[g0 * 4 + hh][1], units[g0 * 4 + hh][0], (g0 % 2) * 4 + hh, hh) for hh in range(4)])
                if stage == 4:
                    for s_ in range(2):
                        dump("oacc%d" % s_, slotv(s_)["oacc"][:], 512, [B("s%d_oacc" % s_)])
                barrier()

            if stage >= 5:
                woutb = aview(0, [128, NKC, 1024], BF16)
                oTsb2 = aview(16384, [128, 16, 512], BF16)
                oTdn2 = aview(32768, [128, 16, 512], BF16)
                mergedT = [aview(49152 + 8192 * t_, [128, 8, 512], BF16) for t_ in range(4)]
                sgt = [[aview(81920 + (3 * r_ + k_) * 2048, [128, 512], F32) for k_ in range(3)] for r_ in range(2)]
                a12 = [aview(94208, [128, 512], F32), aview(96256, [128, 512], F32)]
                xt = [aview(98304, [128, D], F32), aview(102400, [128, D], F32)]
                rt = [aview(106496, [128, D], F32), aview(110592, [128, D], F32)]
                fst = [aview(114688, [128, 4], F32), aview(114704, [128, 4], F32)]
                mhalf = aview(114720, [128, 1], F32)
                Bwoutb, BoTsb2, BoTdn2 = Buf("woutb"), Buf("oTsb2"), Buf("oTdn2")
                BmergedT = [Buf("mergedT%d" % t_) for t_ in range(4)]
                Bsgt = [[Buf("sgt%d%d" % (r_, k_)) for k_ in range(3)] for r_ in range(2)]
                Ba12 = [Buf("a120"), Buf("a121")]
                Bxt = [Buf("xt0"), Buf("xt1")]
                Brt = [Buf("rt0"), Buf("rt1")]
                Bfst = [Buf("fst0"), Buf("fst1")]
                Bmhalf = Buf("mhalf")
                memset('pool', mhalf[:], -0.5, [Bmhalf])
                for kc in range(NKC):
                    P.dma('pool', woutb[:, kc, :], d_wout[:, kc * 1024:(kc + 1) * 1024], writes=[Bwoutb])
                ctr5 = {'g': 0, 'f': 0}

                def merge_pair(p):
                    tgs = (2 * p, 2 * p + 1)
                    for ti, tg in enumerate(tgs):
                        tsl = slice(tg * 512, (tg + 1) * 512)
                        P.dma('sp', oTsb2[:, ti * 8:(ti + 1) * 8, :], d_oTsb[:, :, tsl].rearrange("h p t -> p h t"),
                              reads=[BoT_sb[h_] for h_ in range(8)], writes=[BoTsb2])
                        P.dma('sp', oTdn2[:, ti * 8:(ti + 1) * 8, :], d_oTdn[:, :, tsl].rearrange("h p t -> p h t"),
                              reads=[BoT_dn[h_] for h_ in range(8)], writes=[BoTdn2])
                    for f in range(8):
                        wg = [load_w(d_wmain[blk + f]) for blk in (BLK_GDN, BLK_GSB, BLK_GM)]
                        wdn, Bwdn = load_w(d_wbrdn[f])
                        wsb, Bwsb = load_w(d_wbrsb[f])
                        bi = wctr[1] % NWB
                        wctr[1] += 1
                        P.dma('pool', wbf[bi][:, 0:2, :].rearrange("p a b -> p (a b)"), d_wbrm[f], writes=[Bwbf[bi]])
                        for ti, tg in enumerate(tgs):
                            tsl = slice(tg * 512, (tg + 1) * 512)
                            r_ = ctr5['g'] % 2
                            ctr5['g'] += 1
                            for k_ in range(3):
                                w, Bw = wg[k_]
                                proj_fm(w, Bw, tg, ps[k_], psB[k_])
                                act(sgt[r_][k_][:], ps[k_][:, 0:512], AF.Tanh, [psB[k_]], [Bsgt[r_][k_]], scale=0.5)
                                yield
                            for kc in range(NKC):
                                mm(ps[3][:, 0:512], wdn[:, kc, :], oTdn2[:, ti * 8 + kc, :], kc == 0, kc == NKC - 1, [Bwdn, BoTdn2], [psB[3]])
                            stt(sgt[r_][0][:], sgt[r_][0][:], 1.0, ps[3][:, 0:512], ALU.add, ALU.mult,
                                [Bsgt[r_][0], psB[3]], [Bsgt[r_][0]])
                            yield
                            for kc in range(NKC):
                                mm(ps[4][:, 0:512], wsb[:, kc, :], oTsb2[:, ti * 8 + kc, :], kc == 0, kc == NKC - 1, [Bwsb, BoTsb2], [psB[4]])
                            stt(sgt[r_][1][:], sgt[r_][1][:], 1.0, ps[4][:, 0:512], ALU.add, ALU.mult,
                                [Bsgt[r_][1], psB[4]], [Bsgt[r_][1]])
                            yield
                            for kc in range(2):
                                mm(ps[5][:, 0:512], wbf[bi][:, kc, :], oT_m[:, kc, tsl], kc == 0, kc == 1, [Bwbf[bi], BoT_m], [psB[5]])
                            stt(sgt[r_][2][:], sgt[r_][2][:], 1.0, ps[5][:, 0:512], ALU.add, ALU.mult,
                                [Bsgt[r_][2], psB[5]], [Bsgt[r_][2]])
                            tt('dve', a12[r_][:], sgt[r_][0][:], sgt[r_][1][:], ALU.add, [Bsgt[r_][0], Bsgt[r_][1]], [Ba12[r_]])
                            tt('dve', mergedT[tg][:, f, :], a12[r_][:], sgt[r_][2][:], ALU.add, [Ba12[r_], Bsgt[r_][2]], [BmergedT[tg]])
                            yield

                def final_gen(tg):
                    for jj in range(4):
                        j = tg * 4 + jj
                        fb = ctr5['f'] % 2
                        ctr5['f'] += 1
                        P.dma('sp', xt[fb][:], d_x[j * 128:(j + 1) * 128, :], writes=[Bxt[fb]])
                        for half in range(2):
                            for kc in range(NKC):
                                mm(ps[6 + half][:, 0:512], mergedT[tg][:, kc, jj * 128:(jj + 1) * 128],
                                   woutb[:, kc, half * 512:(half + 1) * 512], kc == 0, kc == NKC - 1,
                                   [BmergedT[tg], Bwoutb], [psB[6 + half]])
                            stt(rt[fb][:, half * 512:(half + 1) * 512], ps[6 + half][:, 0:512], 0.5, xt[fb][:, half * 512:(half + 1) * 512],
                                ALU.mult, ALU.add, [psB[6 + half], Bxt[fb]], [Brt[fb]])
                            yield
                        act(xt[fb][:], rt[fb][:], AF.Square, [Brt[fb], Bxt[fb]], [Bxt[fb], Bfst[fb]], accum=fst[fb][:, 0:1])
                        yield
                        ts('pool', fst[fb][:, 1:2], fst[fb][:, 0:1], 1.0 / D, EPS, ALU.mult, ALU.add, [Bfst[fb]], [Bfst[fb]])
                        tt('pool', fst[fb][:, 2:3], fst[fb][:, 1:2], mhalf[:], ALU.pow, [Bfst[fb], Bmhalf], [Bfst[fb]])
                        yield
                        stt(rt[fb][:], rt[fb][:], fst[fb][:, 2:3], fing[:], ALU.mult, ALU.mult, [Brt[fb], Bfst[fb], cb("fing")], [Brt[fb]])
                        P.dma('sp', d_out[j * 128:(j + 1) * 128, :], rt[fb][:], reads=[Brt[fb]], final=True)
                        yield

                def finals(tgs_):
                    for tg_ in tgs_:
                        for _ in final_gen(tg_):
                            yield

                def run_il5(gens):
                    gens = list(gens)
                    while gens:
                        for g_ in list(gens):
                            try:
                                next(g_)
                            except StopIteration:
                                gens.remove(g_)

                run_il5([merge_pair(0)])
                run_il5([merge_pair(1), finals((0, 1))])
                run_il5([finals((2, 3))])

        except StopBuild:
            pass
        barrier()
        P.emit()
    nc._dbg_items = dbg_items
    return nc


def _consts():
    idx = np.arange(128)
    s_, c_ = idx[:, None], idx[None, :]
    same = (s_ // 64) == (c_ // 64)
    maskb = np.where(same & (c_ >= s_), 0.0, NEGBIG).astype(np.float32)
    offd = (s_ != c_).astype(np.float32)
    causal = (c_ < s_).astype(np.float32)
    esel = np.zeros((8, 8, 128), np.float32)
    for h in range(8):
        esel[h, h, :] = 1.0
    rmask = np.ones((8, 512), np.float32)
    rmask[:, ::64] = 0.0
    return dict(ident=np.eye(128, dtype=np.float32), maskb=maskb, offd=offd, causal=causal,
                esel=esel.reshape(8, 1024), rmask=rmask)


def _layout_shared(inp):
    f = np.float32
    w_in = np.asarray(inp["w_in"][0], f)
    def blocks(w):
        n = w.shape[1] // 128
        return np.ascontiguousarray(w.reshape(8, 128, n, 128).transpose(2, 1, 0, 3).reshape(n, 128, 1024))
    main_cols = np.concatenate([np.arange(0, 4096), np.arange(4112, 11792)])
    wmain = blocks(w_in[:, main_cols])
    wba = np.ascontiguousarray(w_in[:, 4096:4112].reshape(8, 128, 16).transpose(1, 0, 2).reshape(128, 128))
    wmkv = np.ascontiguousarray(np.asarray(inp["w_mem_kv"][0], f).reshape(8, 128, 512).transpose(1, 0, 2).reshape(128, 4096))
    wbrdn = blocks(np.asarray(inp["w_br_dn"][0], f))
    wbrsb = blocks(np.asarray(inp["w_br_sb"][0], f))
    wm = np.asarray(inp["w_br_mem"][0], f)
    wbrm = np.ascontiguousarray(wm.reshape(2, 128, 8, 128).transpose(2, 1, 0, 3).reshape(8, 128, 256))
    wout = np.ascontiguousarray(np.asarray(inp["w_out"][0], f).reshape(8, 128, 1024).transpose(1, 0, 2).reshape(128, 8192))
    convw = np.ascontiguousarray(np.asarray(inp["conv_w"][0], f).reshape(4, 24, 128).transpose(2, 1, 0).reshape(128, 96))
    d = dict(wmain=wmain, wba=wba, wmkv=wmkv, wbrdn=wbrdn, wbrsb=wbrsb, wbrm=wbrm, wout=wout, convw=convw,
             normg=np.ascontiguousarray(np.asarray(inp["norm_g"][0], f).reshape(8, 128).T),
             memg=np.ascontiguousarray(np.asarray(inp["mem_norm_g"][0], f).reshape(8, 128).T),
             fing=np.asarray(inp["final_g"], f).reshape(1, 1024),
             dnng=np.asarray(inp["dn_norm_g"][0], f).reshape(128, 1),
             alog=np.asarray(inp["a_log"][0], f).reshape(8, 1),
             dtb=np.asarray(inp["dt_bias"][0], f).reshape(8, 1))
    d.update(_consts())
    return d


def _layout_core(inp, b):
    f = np.float32
    x = np.asarray(inp["x"][b], f)
    mem = np.asarray(inp["mem"][b], f)
    xT = np.ascontiguousarray(x.T.reshape(8, 128, 2048).transpose(1, 0, 2).reshape(128, 8 * 2048))
    memT = np.ascontiguousarray(mem.T.reshape(8, 128, 256).transpose(1, 0, 2).reshape(128, 8 * 256))
    return dict(xT=xT, x=np.ascontiguousarray(x), memT=memT)


def kernel(**inputs):
    shared = _layout_shared(inputs)
    in_maps = []
    for b in range(8):
        m = dict(shared)
        m.update(_layout_core(inputs, b))
        in_maps.append(m)
    nc = build_nc()
    res = run_bass_kernel_spmd(nc, in_maps, core_ids=list(range(8)))
    out = np.stack([np.asarray(res.results[b]["out"], np.float32) for b in range(8)], axis=0)
    return out
```
